# Optimizing a Trainium2 kernel written in Bass

```python
import jax, jax.numpy as jnp
from jax import lax
import numpy as np

D_MODEL = 2048
BATCH = 4
SEQ = 2048
DEPTH = 1

CHUNK = 64
MIX_WIDTH = D_MODEL
ATT_WIDTH = MIX_WIDTH // 2
HG_WIDTH = MIX_WIDTH - ATT_WIDTH
ATT_HEAD_DIM = 128
ATT_HEADS = ATT_WIDTH // ATT_HEAD_DIM
ATT_LEFT_CHUNKS = 8
ATT_BAND = ATT_LEFT_CHUNKS + 1
MAX_REL = 128
HG_EXPAND = 128
HG_HEADS = HG_WIDTH // HG_EXPAND
HG_KDIM = HG_EXPAND
HG_FDIM = HG_HEADS * HG_KDIM
HG_VDIM = HG_WIDTH // HG_HEADS
IN_SPLITS = (ATT_WIDTH, ATT_WIDTH, ATT_WIDTH, ATT_WIDTH, HG_FDIM, HG_FDIM, HG_WIDTH, HG_WIDTH)
IN_COLS = sum(IN_SPLITS)
EPS = 1e-6

kernel_name = "hymba_chunkattn_hgrn2_adaln_block"


def _rms(x, g):
    xf = x.astype(jnp.float32)
    y = xf * lax.rsqrt(jnp.mean(xf * xf, axis=-1, keepdims=True) + EPS)
    return (y * g.astype(jnp.float32)).astype(x.dtype)


def _chunk_attention(q, k, v, q_g, k_g, rel_bias):
    B, T, H, dh = q.shape
    N = T // CHUNK
    q = _rms(q, q_g).reshape(B, N, CHUNK, H, dh)
    k = _rms(k, k_g).reshape(B, N, CHUNK, H, dh)
    v = v.reshape(B, N, CHUNK, H, dh)
    pad = ((0, 0), (ATT_LEFT_CHUNKS, 0), (0, 0), (0, 0), (0, 0))
    kp = jnp.pad(k, pad)
    vp = jnp.pad(v, pad)
    band_idx = jnp.arange(N)[:, None] + jnp.arange(ATT_BAND)[None, :]
    kb = kp[:, band_idx].reshape(B, N, ATT_BAND * CHUNK, H, dh)
    vb = vp[:, band_idx].reshape(B, N, ATT_BAND * CHUNK, H, dh)
    scores = jnp.einsum('bnqhd,bnkhd->bnhqk', q, kb).astype(jnp.float32) * (dh ** -0.5)
    qi = jnp.arange(CHUNK)
    kj = jnp.arange(ATT_BAND * CHUNK)
    dist = ATT_LEFT_CHUNKS * CHUNK + qi[:, None] - kj[None, :]
    bias_idx = jnp.clip(dist, -MAX_REL, MAX_REL) + MAX_REL
    bias = rel_bias.astype(jnp.float32)[:, bias_idx]
    valid = (jnp.arange(N)[:, None] + kj[None, :] // CHUNK) >= ATT_LEFT_CHUNKS
    scores = jnp.where(valid[None, :, None, None, :], scores + bias[None, None], -jnp.inf)
    p = jax.nn.softmax(scores, axis=-1).astype(v.dtype)
    out = jnp.einsum('bnhqk,bnkhd->bnqhd', p, vb)
    return out.reshape(B, T, H * dh)


def _hgrn2(q_raw, f_raw, i, lb, o_g):
    B, T, _ = q_raw.shape
    N = T // CHUNK
    out_dtype = i.dtype
    f32 = jnp.float32
    q = jax.nn.silu(q_raw.astype(f32))
    fr = f_raw.astype(f32)
    lb = lb.astype(f32)
    logf = jnp.log(lb + (1.0 - lb) * jax.nn.sigmoid(fr))
    k = (1.0 - lb) * jax.nn.sigmoid(-fr)
    v = i.astype(f32)

    def to_chunks(a, d):
        return a.reshape(B, N, CHUNK, HG_HEADS, d).transpose(1, 0, 3, 2, 4)

    qc, kc, lc = to_chunks(q, HG_KDIM), to_chunks(k, HG_KDIM), to_chunks(logf, HG_KDIM)
    vc = to_chunks(v, HG_VDIM)
    tril = jnp.arange(CHUNK)[:, None] >= jnp.arange(CHUNK)[None, :]

    def step(S, xs):
        qn, kn, vn, ln = xs
        b = jnp.cumsum(ln, axis=2)
        b_last = b[:, :, -1]
        inter = jnp.einsum('bhcd,bhde->bhce', qn * jnp.exp(b), S)
        diff = b[:, :, :, None, :] - b[:, :, None, :, :]
        decay = jnp.exp(jnp.where(tril[None, None, :, :, None], diff, -jnp.inf))
        A = jnp.einsum('bhtd,bhtsd,bhsd->bhts', qn, decay, kn)
        intra = jnp.einsum('bhts,bhse->bhte', A, vn)
        k_dec = kn * jnp.exp(b_last[:, :, None, :] - b)
        S_new = jnp.exp(b_last)[..., None] * S + jnp.einsum('bhsd,bhse->bhde', k_dec, vn)
        return S_new, inter + intra

    S0 = jnp.zeros((B, HG_HEADS, HG_KDIM, HG_VDIM), f32)
    _, o = lax.scan(step, S0, (qc, kc, vc, lc))
    o = o.transpose(1, 0, 3, 2, 4).reshape(B, T, HG_HEADS, HG_VDIM)
    o = _rms(o, o_g)
    return o.reshape(B, T, HG_WIDTH).astype(out_dtype)


def setup_inputs(seed: int = 0) -> dict:
    key = jax.random.key(seed)
    ks = jax.random.split(key, 16)
    D = D_MODEL
    f32 = jnp.float32
    nrm = lambda k, s: jax.random.normal(k, s, f32)
    return {
        "x": nrm(ks[0], (BATCH, SEQ, D)),
        "c": nrm(ks[1], (BATCH, D)),
        "norm_g": 1.0 + 0.1 * nrm(ks[2], (DEPTH, D)),
        "w_ada": nrm(ks[3], (DEPTH, D, 3 * D)) * D ** -0.5,
        "b_ada": 0.01 * nrm(ks[4], (DEPTH, 3 * D)),
        "w_in": nrm(ks[5], (DEPTH, D, IN_COLS)) * D ** -0.5,
        "q_norm_g": 1.0 + 0.1 * nrm(ks[6], (DEPTH, ATT_HEAD_DIM)),
        "k_norm_g": 1.0 + 0.1 * nrm(ks[7], (DEPTH, ATT_HEAD_DIM)),
        "rel_bias": 0.5 * nrm(ks[8], (DEPTH, ATT_HEADS, 2 * MAX_REL + 1)),
        "lower_bounds": nrm(ks[9], (DEPTH + 1, HG_FDIM)),
        "hg_norm_g": 1.0 + 0.1 * nrm(ks[10], (DEPTH, HG_VDIM)),
        "w_out": nrm(ks[11], (DEPTH, MIX_WIDTH, D)) * MIX_WIDTH ** -0.5,
    }


def reference(x, c, norm_g, w_ada, b_ada, w_in, q_norm_g, k_norm_g, rel_bias,
              lower_bounds, hg_norm_g, w_out):
    B, T, D = x.shape
    lbs = jnp.cumsum(jax.nn.softmax(lower_bounds.astype(jnp.float32), axis=0), axis=0)
    offsets = np.cumsum(IN_SPLITS)[:-1].tolist()
    c_act = jax.nn.silu(c)
    for l in range(DEPTH):
        mod = c_act @ w_ada[l] + b_ada[l]
        shift, scale, gate = jnp.split(mod, 3, axis=-1)
        h = _rms(x, norm_g[l]) * (1.0 + scale[:, None, :]) + shift[:, None, :]
        proj = h @ w_in[l]
        a_q, a_k, a_v, a_z, g_q, g_f, g_i, g_z = jnp.split(proj, offsets, axis=-1)
        hs = (B, T, ATT_HEADS, ATT_HEAD_DIM)
        att = _chunk_attention(a_q.reshape(hs), a_k.reshape(hs), a_v.reshape(hs),
                               q_norm_g[l], k_norm_g[l], rel_bias[l])
        att = att * jax.nn.silu(a_z)
        hg = _hgrn2(g_q, g_f, g_i, lbs[l], hg_norm_g[l]) * jax.nn.silu(g_z)
        y = jnp.concatenate([att, hg], axis=-1) @ w_out[l]
        x = x + gate[:, None, :] * y
    return x
```

```python
import numpy as np
from contextlib import ExitStack
import concourse.bass as bass
import concourse.mybir as mybir
from concourse.bass_utils import run_bass_kernel_spmd

F32 = mybir.dt.float32
BF16 = mybir.dt.bfloat16
AF = mybir.ActivationFunctionType
ALU = mybir.AluOpType

D = 2048
T = 2048
NT = 16
NCH = 16
NH = 2
WD = NH * 128
NPASS = 2
EPS = 1e-6
NEG = -30000.0

CF_M1 = 0
CF_M2 = CF_M1 + WD
CF_RB = CF_M2 + WD
CF_MASK = CF_RB + 2
CF_OG = CF_MASK + 640
CF_LB0 = CF_OG + 512
CF_LB1 = CF_LB0 + 512
CF_NG = CF_LB1 + 512
CF_GQ = CF_NG + 16
CF_GK = CF_GQ + 1
CF_C = CF_GK + 1
CF_ONE = CF_C + 16
NF = CF_ONE + 128
CB_ID = 0
CB_ONE = 128
CB_L = 256
CB_IND = CB_L + 5 * 128
NB = CB_IND + 2 + 2


class Sched:
    def __init__(self):
        self.ops = []
        self.lastw = {}
        self.readers = {}
        self.gflag = False
        self.off = False
        self.budget = None

    BANK = {'psq': 'P0', 'psf': 'P0', 'psi': 'P1', 'psz': 'P1', 'pe_inter': 'P2', 'pe_kdec': 'P2',
            'pe_q1': 'P3', 'pe_k1': 'P3', 'pe_q2': 'P4', 'psbl': 'P4', 'pt_qi': 'P5', 'pt_q1': 'P5',
            'pt_k1': 'P5', 'pt_q2': 'P6', 'pt_k2': 'P4', 'pgt': 'P6', 'psA1': 'P6', 'pst': 'P6',
            'psA2': 'P7', 'pso': 'P7', 'po0': 'P7', 'po1': 'P7', 'P0v': 'P0', 'P0z': 'P0',
            'P1v': 'P1', 'P1z': 'P1'}

    def add(self, eng, fn, r=(), w=(), kind='c', g=None):
        if self.off:
            return -1
        if self.budget is not None:
            if self.budget <= 0:
                return -1
            self.budget -= 1
        i = len(self.ops)
        r = [self.BANK.get(k, k) for k in r]
        w = [self.BANK.get(k, k) for k in w]
        if (self.gflag if g is None else g):
            r.append('GPH')
        deps = set()
        for k in r:
            lw = self.lastw.get(k)
            if lw is not None:
                deps.add(lw)
        for k in w:
            lw = self.lastw.get(k)
            if lw is not None:
                deps.add(lw)
            rd = self.readers.get(k)
            if rd:
                deps.update(rd[0].values())
                deps.update(rd[1])
        deps.discard(i)
        self.ops.append((eng, fn, deps, kind))
        for k in r:
            rd = self.readers.setdefault(k, ({}, []))
            if kind == 'c':
                rd[0][eng] = i
            else:
                rd[1].append(i)
        for k in w:
            self.lastw[k] = i
            self.readers[k] = ({}, [])
        return i

    def barrier(self):
        pass

    def emit(self, block, sems, dma_sems, cc_sems):
        ops = self.ops
        n = len(ops)
        need = [False] * n
        for i, (eng, fn, deps, kind) in enumerate(ops):
            for d in deps:
                de, _, _, dk = ops[d]
                if de == 'pe' and eng == 'pe' and dk == 'c' and kind == 'c':
                    continue
                need[d] = True
        sig = {}
        pre = {}
        cnt = {}
        rr = 0
        rrp = 0
        dcnt = [0] * len(dma_sems)
        cci = 0
        for i, (eng, fn, deps, kind) in enumerate(ops):
            if kind == 'dma':
                if eng == 'pool':
                    k = 16 + (rrp % 8)
                    rrp += 1
                else:
                    k = rr % 16
                    rr += 1
                if dcnt[k] > 0:
                    pre[i] = (dma_sems[k], 16 * dcnt[k])
                dcnt[k] += 1
                sig[i] = (dma_sems[k], 16 * dcnt[k])
            elif kind == 'cc':
                sig[i] = (cc_sems[cci], 1)
                cci += 1
            elif need[i]:
                cnt[eng] = cnt.get(eng, 0) + 1
                sig[i] = (sems[eng], cnt[eng])
        order = {}
        for i, o in enumerate(ops):
            order.setdefault(o[0], []).append(i)

        def run(engname, e):
            waited = {}
            for i in order.get(engname, []):
                eng, fn, deps, kind = ops[i]
                wl = {}
                for d in deps:
                    if d not in sig:
                        continue
                    de, _, _, dk = ops[d]
                    if de == 'pe' and eng == 'pe' and dk == 'c' and kind == 'c':
                        continue
                    sm, val = sig[d]
                    key = id(sm)
                    if waited.get(key, 0) < val:
                        if key not in wl or wl[key][1] < val:
                            wl[key] = (sm, val)
                if i in pre:
                    sm, val = pre[i]
                    key = id(sm)
                    if waited.get(key, 0) < val:
                        if key not in wl or wl[key][1] < val:
                            wl[key] = (sm, val)
                for key, (sm, val) in wl.items():
                    e.wait_ge(sm, val)
                    waited[key] = val
                ins = fn(e)
                if i in sig:
                    sm, val = sig[i]
                    if kind == 'dma':
                        ins.then_inc(sm, 16)
                    elif kind == 'cc':
                        ins.then_inc(sm)
                    else:
                        ins.then_inc(sm, 1)
            if engname == 'sp':
                for k, sm in enumerate(dma_sems):
                    if dcnt[k] > 0:
                        e.wait_ge(sm, 16 * dcnt[k])
                for k in range(cci):
                    e.wait_ge(cc_sems[k], 1)

        @block.tensor
        def _(e):
            run('pe', e)

        @block.scalar
        def _(e):
            run('act', e)

        @block.vector
        def _(e):
            run('dve', e)

        @block.gpsimd
        def _(e):
            run('pool', e)

        @block.sync
        def _(e):
            run('sp', e)


DEBUG = False
STOP = 99
SUB = 0
CUT = -1


def build_program():
    nc = bass.Bass("TRN2", target_bir_lowering=False)
    S = Sched()
    dbg_h = nc.dram_tensor("dbg_h", [128, NCH, T], BF16) if DEBUG else None

    def din(name, shape):
        return nc.dram_tensor(name, shape, F32, kind="ExternalInput").ap()

    x_b = din("x_b", [T, D])
    x_res = din("x_res", [T, 1024])
    w_ada_c = din("w_ada_c", [D, 5120])
    b_ada_c = din("b_ada_c", [1, 5120])
    w_in_c = din("w_in_c", [8, D, 512])
    w_out_c = din("w_out_c", [D, 1024])
    cbf_d = din("cbf", [128, NB])
    cf_d = din("cf", [128, NF])
    bias_d = din("biasT", [128, 4 * 640])
    out_d = nc.dram_tensor("out", [T, 1024], F32, kind="ExternalOutput").ap()
    ibs = [nc.dram_tensor(f"ib{i}", [128, T], BF16) for i in range(8)]
    obs = [nc.dram_tensor(f"ob{i}", [256, T], BF16) for i in range(8)]

    es = ExitStack()
    with es:
        def sb(name, shape, dt):
            return es.enter_context(nc.sbuf_tensor(name, shape, dt))

        def ps(name):
            return es.enter_context(nc.psum_tensor(name, [128, 512], F32))

        T_h = sb("T_h", [128, NCH, T], BF16)
        T_w = [sb(f"T_w{i}", [128, NCH, WD], BF16) for i in range(4)]
        cbf = sb("cbf_s", [128, NB], BF16)
        cf = sb("cf_s", [128, NF], F32)
        GSZ = 31 * 1024
        G = sb("G", [128, GSZ], BF16)
        modT = sb("modT", [128, 32], F32)
        sc1 = sb("sc1", [128, 16], F32)
        gate_bc = sb("gate_bc", [128, 1024], F32)
        cact = sb("cact", [128, 16], BF16)
        small = sb("small", [128, 64], F32)
        lb_bc = sb("lb_bc", [128, 512], F32)
        oml_bc = sb("oml_bc", [128, 512], F32)
        rowblk = sb("rowblk", [1, 512], F32)
        bblk = sb("bblk", [1, 512], F32)
        scratch = sb("scratch", [128, 4], F32)
        P = [ps(f"P{i}") for i in range(8)]

        sems = {k: es.enter_context(nc.semaphore("s_" + k)) for k in ('pe', 'act', 'dve', 'pool', 'sp')}
        dma_sems = [es.enter_context(nc.semaphore(f"dq{i}")) for i in range(24)]
        cc_sems = [es.enter_context(nc.semaphore(f"cc{i}")) for i in range(8)]
        block = es.enter_context(nc.Block())

        ident = cbf[:, CB_ID:CB_ID + 128]
        ones_bf = cbf[:, CB_ONE:CB_ONE + 128]

        def Lm(i):
            return cbf[:, CB_L + i * 128: CB_L + (i + 1) * 128]
        ind2 = cbf[:, CB_IND:CB_IND + 2]

        def mm(out, lhsT, rhs, start, stop, r, w):
            S.add('pe', lambda e: e.matmul(out, lhsT, rhs, start=start, stop=stop), r, w)

        def tr(out, in_, r, w):
            S.add('pe', lambda e: e.transpose(out, in_, ident), list(r) + ['cbf'], w)

        def act(out, in_, func, r, w, bias=None, scale=None, accum=None):
            kw = {}
            if bias is not None:
                kw['bias'] = bias
            if scale is not None:
                kw['scale'] = scale
            if accum is not None:
                kw['accum_out'] = accum
            S.add('act', lambda e: e.activation(out, in_, func, **kw), r, w)

        def tt(eng, out, a, b, op, r, w):
            S.add(eng, lambda e: e.tensor_tensor(out, a, b, op), r, w)

        def ts(eng, out, a, s1, s2, op0, op1, r, w):
            if op1 is None:
                S.add(eng, lambda e: e.tensor_scalar(out, a, s1, None, op0=op0), r, w)
            else:
                S.add(eng, lambda e: e.tensor_scalar(out, a, s1, s2, op0=op0, op1=op1), r, w)

        def stt(eng, out, a, sc, b, op0, op1, r, w):
            S.add(eng, lambda e: e.scalar_tensor_tensor(out, a, sc, b, op0=op0, op1=op1), r, w)

        def cp(eng, out, in_, r, w):
            if eng == 'act':
                S.add('act', lambda e: e.copy(out, in_), r, w)
            else:
                S.add(eng, lambda e: e.tensor_copy(out, in_), r, w)

        def rcp(out, in_, r, w):
            S.add('dve', lambda e: e.reciprocal(out, in_), r, w)

        def dma(q, out, in_, r, w, g=None):
            S.add(q, lambda e: e.dma_start(out=out, in_=in_), r, w, kind='dma', g=g)

        def phase_barrier():
            S.add('dve', lambda e: e.memset(scratch[:, 0:1], 0.0), r=(), w=('GPH',), g=False)

        class Carve:
            def __init__(self):
                self.off = 0

            def get(self, free, dt):
                n = 1
                for f in free:
                    n *= f
                nb = n * (2 if dt == BF16 else 4)
                nb = (nb + 3) // 4 * 4
                a = self.off // 2
                assert a + nb // 2 <= GSZ, (a, nb, GSZ)
                v = G[:, a:a + nb // 2]
                self.off += nb
                if dt == F32:
                    v = v.bitcast(F32)
                if len(free) == 2:
                    v = v.rearrange("p (a b) -> p a b", b=free[1])
                elif len(free) == 3:
                    v = v.rearrange("p (a b c) -> p a b c", b=free[1], c=free[2])
                return v

        dma('pool', cbf[:], cbf_d[:, :], [], ['cbf'], g=False)
        dma('sp', cf[:], cf_d[:, :], [], ['cf'], g=False)

        tt('dve', lb_bc[:], cf[:, CF_LB0:CF_LB0 + 512], cf[:, CF_LB1:CF_LB1 + 512], ALU.subtract, ['cf'], ['lb'])
        act(lb_bc[:], lb_bc[:], AF.Exp, ['lb'], ['lb'], scale=-1.0)
        ts('dve', lb_bc[:], lb_bc[:], 1.0, None, ALU.add, None, ['lb'], ['lb'])
        rcp(lb_bc[:], lb_bc[:], ['lb'], ['lb'])
        ts('dve', oml_bc[:], lb_bc[:], -1.0, 1.0, ALU.mult, ALU.add, ['lb'], ['oml'])
        ts('dve', small[:, 0:1], cf[:, CF_GQ:CF_GQ + 1], 128.0 ** -0.5, None, ALU.mult, None, ['cf'], ['gq'])
        cp('dve', small[:, 1:2], cf[:, CF_GK:CF_GK + 1], ['cf'], ['gk'])
        ts('dve', small[:, 2:3], cf[:, CF_GK:CF_GK + 1], 0.0, EPS, ALU.mult, ALU.add, ['cf'], ['epsc'])
        eps_c = small[:, 2:3]

        cT = cf[:, CF_C:CF_C + 16]
        act(small[:, 16:32], cT, AF.Exp, ['cf'], ['ctmp'], scale=-1.0)
        ts('dve', small[:, 16:32], small[:, 16:32], 1.0, None, ALU.add, None, ['ctmp'], ['ctmp'])
        rcp(small[:, 16:32], small[:, 16:32], ['ctmp'], ['ctmp'])
        tt('dve', cact[:], small[:, 16:32], cT, ALU.mult, ['ctmp', 'cf'], ['cact'])
        ones_f = cf[:, CF_ONE:CF_ONE + 128]
        wada_v = w_ada_c.rearrange("(c p) n -> p c n", p=128)
        for blk in range(20):
            wb = T_w[blk % 4]
            wk = f'Tw{blk % 4}'
            for q4 in range(4):
                dma('pool', wb[:, q4 * 4:(q4 + 1) * 4, :], wada_v[:, q4 * 4:(q4 + 1) * 4, blk * 256:(blk + 1) * 256], [], [wk], g=False)
            dma('sp', bblk[0:1, 0:256], b_ada_c[0:1, blk * 256:(blk + 1) * 256], [], ['bblk'], g=False)
            for c in range(NCH):
                mm(P[0][0:1, 0:256], cact[:, c:c + 1], wb[:, c, :], c == 0, c == NCH - 1, [wk, 'cact'], ['P0'])
            tt('dve', rowblk[0:1, 0:256], P[0][0:1, 0:256], bblk[0:1, 0:256], ALU.add, ['P0', 'bblk'], ['rowblk'])
            if blk < 16:
                for s in range(2):
                    col = blk * 2 + s
                    mm(P[1][:, col:col + 1], rowblk[0:1, s * 128:(s + 1) * 128], ones_f[0:1, 0:1], True, True,
                       ['rowblk', 'cf'], ['P1'])
            else:
                gb = blk - 16
                mm(P[2][:, 0:256], ones_f[0:1, 0:128], rowblk[0:1, 0:256], True, True, ['rowblk', 'cf'], ['P2'])
                cp('dve', gate_bc[:, gb * 256:(gb + 1) * 256], P[2][:, 0:256], ['P2'], ['gate'])
            if blk == 15:
                cp('dve', modT[:], P[1][:, 0:32], ['P1'], ['modT'])
        stt('dve', sc1[:], modT[:, 16:32], 1.0, cf[:, CF_NG:CF_NG + 16], ALU.add, ALU.mult, ['modT', 'cf'], ['sc1'])

        if STOP < 1:
            S.off = True
        S.gflag = True
        phase_barrier()
        cv = Carve()
        xs = [cv.get([D], F32) for _ in range(2)]
        xn = cv.get([4, D], BF16)
        junk = cv.get([D], BF16)
        ssum = small[:, 32:48]
        S.add('dve', lambda e: e.memset(ssum, 0.0), [], [f'ss{i}' for i in range(16)])
        rstd = small[:, 48:64]
        for gI in range(4):
            for ti in range(4):
                Tt = gI * 4 + ti
                xb_ = xs[Tt % 2]
                xk = f'xs{Tt % 2}'
                dma('sp', xb_, x_b[Tt * 128:(Tt + 1) * 128, :], [], [xk])
                act(junk, xb_, AF.Square, [xk], ['junk', f'ss{Tt}'], accum=ssum[:, Tt:Tt + 1])
                act(rstd[:, Tt:Tt + 1], ssum[:, Tt:Tt + 1], AF.Ln, [f'ss{Tt}', 'epsc'], [f'rs{Tt}'], bias=eps_c, scale=1.0 / D)
                act(rstd[:, Tt:Tt + 1], rstd[:, Tt:Tt + 1], AF.Exp, [f'rs{Tt}'], [f'rs{Tt}'], scale=-0.5)
                ts('dve', xn[:, ti, :], xb_, rstd[:, Tt:Tt + 1], None, ALU.mult, None, [xk, f'rs{Tt}'], [f'xn{ti}'])
            for c in range(NCH):
                pb = P[c % 4]
                pk = f'P{c % 4}'
                pbv = pb[:, 0:256].bitcast(BF16)
                for ti in range(4):
                    tr(pbv[:, ti * 128:(ti + 1) * 128], xn[:, ti, c * 128:(c + 1) * 128], [f'xn{ti}'], [pk])
                dst = T_h[:, c, gI * 512:(gI + 1) * 512]
                if c % 2 == 0:
                    act(dst, pbv, AF.Identity, [pk, 'sc1', 'modT'], [f'h{c}'], bias=modT[:, c:c + 1], scale=sc1[:, c:c + 1])
                else:
                    ts('dve', dst, pbv, sc1[:, c:c + 1], modT[:, c:c + 1], ALU.mult, ALU.add, [pk, 'sc1', 'modT'], [f'h{c}'])
        hkeys = [f'h{c}' for c in range(NCH)]
        if DEBUG:
            dma('sp', dbg_h[:, :, :], T_h[:, :, :], hkeys, ['dbgh'])

        def load_slab(slot, s_idx, p):
            src = w_in_c[s_idx].rearrange("(c p) n -> p c n", p=128)
            for q4 in range(4):
                dma('pool', T_w[slot][:, q4 * 4:(q4 + 1) * 4, :], src[:, q4 * 4:(q4 + 1) * 4, p * WD:(p + 1) * WD], [], [f'Tw{slot}'], g=False)

        if STOP < 2:
            S.off = True
        if CUT >= 0:
            S.budget = CUT
        phase_barrier()
        cv = Carve()
        f32t = {k: cv.get([WD], F32) for k in ('tq', 'qf', 'tf', 'ff', 'logf', 'kk', 'tz', 'gz2', 'E', 'A1', 'A2')}
        bft = {k: cv.get([WD], BF16) for k in ('hi', 'lo', 'vb', 'qi', 'kd', 'q1', 'k1', 'q2', 'k2', 'ghg')}
        ATb = cv.get([NH, 128], BF16)
        trT = {k: cv.get([NH, 128], BF16) for k in ('qiT0', 'qiT1', 'q1T', 'k1T', 'q2T', 'k2T', 'ghgT')}
        Sf = cv.get([NH, 128], F32)
        Sb = cv.get([NH, 128], BF16)
        Dd = cv.get([2 * NH], F32)
        ssq = cv.get([NH], F32)
        rso = cv.get([NH], F32)
        junk2 = cv.get([128], BF16)
        rbq1 = cf[:, CF_RB:CF_RB + 1]
        rbk1 = cf[:, CF_RB + 1:CF_RB + 2]
        M1 = cf[:, CF_M1:CF_M1 + WD]
        M2 = cf[:, CF_M2:CF_M2 + WD]

        def sigm(dst, src_ps, tmpk, tmp, pk, outk):
            act(tmp, src_ps, AF.Exp, [pk], [tmpk], scale=-1.0)
            ts('dve', tmp, tmp, 1.0, None, ALU.add, None, [tmpk], [tmpk])
            rcp(dst, tmp, [tmpk], [outk])

        for p in range(NPASS):
            for s in range(4):
                load_slab(s, 4 + s, p)
            S.add('dve', lambda e: e.memset(trT['qiT0'][:, :, :], 0.0), [], ['qiT0'])
            S.add('dve', lambda e: e.memset(trT['qiT1'][:, :, :], 0.0), [], ['qiT1'])
            S.add('dve', lambda e: e.memset(Sf[:, :, :], 0.0), [], ['Sf'])
            S.add('dve', lambda e: e.memset(Sb[:, :, :], 0.0), [], ['Sb'])
            lbp = lb_bc[:, p * WD:(p + 1) * WD]
            omlp = oml_bc[:, p * WD:(p + 1) * WD]
            ogp = cf[:, CF_OG + p * WD: CF_OG + (p + 1) * WD]
            for Tt in range(NT if SUB < 3 else SUB - 2):
                tok = slice(Tt * 128, (Tt + 1) * 128)
                psq, psf = P[0][:, 0:WD], P[0][:, 256:256 + WD]
                psi, psz = P[1][:, 0:WD], P[1][:, 256:256 + WD]
                for (dst, dk, slot) in ((psq, 'psq', 0), (psf, 'psf', 1), (psi, 'psi', 2), (psz, 'psz', 3)):
                    for c in range(NCH):
                        mm(dst, T_h[:, c, tok], T_w[slot][:, c, :], c == 0, c == NCH - 1, [f'h{c}', f'Tw{slot}'], [dk])
                sigm(f32t['tq'], psq, 'tq', f32t['tq'], 'psq', 'tq')
                tt('dve', f32t['qf'], psq, f32t['tq'], ALU.mult, ['psq', 'tq'], ['qf'])
                sigm(f32t['tf'], psf, 'tf', f32t['tf'], 'psf', 'tf')
                tt('dve', f32t['ff'], f32t['tf'], omlp, ALU.mult, ['tf', 'oml'], ['ff'])
                tt('dve', f32t['ff'], f32t['ff'], lbp, ALU.add, ['ff', 'lb'], ['ff'])
                act(f32t['logf'], f32t['ff'], AF.Ln, ['ff'], ['logf'])
                ts('dve', f32t['kk'], f32t['ff'], -1.0, 1.0, ALU.mult, ALU.add, ['ff'], ['kk'])
                cp('dve', bft['hi'], f32t['logf'], ['logf'], ['hi'])
                tt('dve', bft['lo'], f32t['logf'], bft['hi'], ALU.subtract, ['logf', 'hi'], ['lo'])
                cp('act', bft['vb'], psi, ['psi'], ['vb'])
                sigm(f32t['tz'], psz, 'tz', f32t['tz'], 'psz', 'tz')
                tt('dve', f32t['gz2'], psz, f32t['tz'], ALU.mult, ['psz', 'tz'], ['gz2'])
                tt('dve', f32t['gz2'], f32t['gz2'], ogp, ALU.mult, ['gz2', 'cf'], ['gz2'])
                pes = {'inter': (P[2][:, 0:WD], 0), 'kdec': (P[2][:, 256:256 + WD], 1), 'q1': (P[3][:, 0:WD], 2),
                       'k1': (P[3][:, 256:256 + WD], 3), 'q2': (P[4][:, 0:WD], 4)}
                for nm, (pt, li) in pes.items():
                    mm(pt, Lm(li), bft['hi'], True, False, ['hi', 'cbf'], ['pe_' + nm])
                    mm(pt, Lm(li), bft['lo'], False, True, ['lo', 'cbf'], ['pe_' + nm])
                psbl = P[4][:, 256:256 + 2 * NH]
                for h in range(NH):
                    mm(psbl[:, 2 * h:2 * h + 2], bft['hi'][:, h * 128:(h + 1) * 128], ind2, True, False, ['hi', 'cbf'], ['psbl'])
                    mm(psbl[:, 2 * h:2 * h + 2], bft['lo'][:, h * 128:(h + 1) * 128], ind2, False, True, ['lo', 'cbf'], ['psbl'])
                act(Dd, psbl, AF.Exp, ['psbl'], ['Dd'])

                def expmul(pe_nm, src_k, src, dst_k, bias=None, scale=None):
                    act(f32t['E'], pes[pe_nm][0], AF.Exp, ['pe_' + pe_nm, 'cf'], ['E'], bias=bias, scale=scale)
                    tt('dve', bft[dst_k], src, f32t['E'], ALU.mult, [src_k, 'E'], [dst_k])
                expmul('inter', 'qf', f32t['qf'], 'qi')
                expmul('kdec', 'kk', f32t['kk'], 'kd')
                expmul('q1', 'qf', f32t['qf'], 'q1', bias=rbq1)
                expmul('k1', 'kk', f32t['kk'], 'k1', bias=rbk1)
                expmul('q2', 'qf', f32t['qf'], 'q2')
                expmul('q2', 'kk', f32t['kk'], 'k2', scale=-1.0)
                P5b = P[5][:, :].bitcast(BF16)
                P6b = P[6][:, 0:128].bitcast(BF16)
                P4b = P[4][:, 384:512].bitcast(BF16)
                tslots = {'qi': P5b[:, 0:WD], 'q1': P5b[:, 256:256 + WD], 'k1': P5b[:, 512:512 + WD],
                          'q2': P6b[:, 0:WD], 'k2': P4b[:, 0:WD]}
                for nm, pt in tslots.items():
                    for h in range(NH):
                        tr(pt[:, h * 128:(h + 1) * 128], bft[nm][:, h * 128:(h + 1) * 128], [nm], ['pt_' + nm])
                ptq = tslots['qi'].rearrange("p (a b) -> p a b", b=128)
                cp('dve', trT['qiT0'][:, :, 0:64], ptq[:, :, 0:64], ['pt_qi'], ['qiT0'])
                cp('dve', trT['qiT1'][:, :, 64:128], ptq[:, :, 64:128], ['pt_qi'], ['qiT1'])
                for nm in ('q1', 'k1', 'q2', 'k2'):
                    eng = 'act' if nm in ('q1', 'q2') else 'dve'
                    cp(eng, trT[nm + 'T'][:, :, :], tslots[nm].rearrange("p (a b) -> p a b", b=128), ['pt_' + nm], [nm + 'T'])
                psA1 = P[6][:, 256:256 + WD]
                psA2 = P[7][:, 0:WD]
                pso = P[7][:, 256:256 + WD]
                for h in range(NH):
                    hs = slice(h * 128, (h + 1) * 128)
                    mm(psA1[:, hs], trT['k1T'][:, h, :], trT['q1T'][:, h, :], True, True, ['k1T', 'q1T'], ['psA1'])
                    mm(psA2[:, hs], trT['k2T'][:, h, :], trT['q2T'][:, h, :], True, True, ['k2T', 'q2T'], ['psA2'])
                tt('dve', f32t['A1'], psA1, M1, ALU.mult, ['psA1', 'cf'], ['A1'])
                tt('dve', f32t['A2'], psA2, M2, ALU.mult, ['psA2', 'cf'], ['A2'])
                tt('dve', ATb[:, :, :], f32t['A1'].rearrange("p (a b) -> p a b", b=128),
                   f32t['A2'].rearrange("p (a b) -> p a b", b=128), ALU.add, ['A1', 'A2'], ['ATb'])
                psU = [P[2][:, 0:WD], P[2][:, 256:256 + WD]]
                for h in range(NH):
                    hs = slice(h * 128, (h + 1) * 128)
                    mm(pso[:, hs], ATb[:, h, :], bft['vb'][:, hs], True, False, ['ATb', 'vb'], ['pso'])
                    mm(pso[:, hs], trT['qiT0'][:, h, :], Sb[:, h, :], False, False, ['qiT0', f'Sb{h}'], ['pso'])
                    mm(psU[0][:, hs], bft['kd'][0:64, hs], bft['vb'][0:64, hs], True, True, ['kd', 'vb'], ['pe_inter'])
                    stt('dve', Sf[:, h, :], Sf[:, h, :], Dd[:, 2 * h:2 * h + 1], psU[0][:, hs], ALU.mult, ALU.add,
                        ['pe_inter', 'Dd', f'Sf{h}'], [f'Sf{h}'])
                    cp('act', Sb[:, h, :], Sf[:, h, :], [f'Sf{h}'], [f'Sb{h}'])
                    mm(pso[:, hs], trT['qiT1'][:, h, :], Sb[:, h, :], False, True, ['qiT1', f'Sb{h}'], ['pso'])
                    mm(psU[1][:, hs], bft['kd'][64:128, hs], bft['vb'][64:128, hs], True, True, ['kd', 'vb'], ['pe_kdec'])
                    stt('dve', Sf[:, h, :], Sf[:, h, :], Dd[:, 2 * h + 1:2 * h + 2], psU[1][:, hs], ALU.mult, ALU.add,
                        ['pe_kdec', 'Dd', f'Sf{h}'], [f'Sf{h}'])
                    cp('act', Sb[:, h, :], Sf[:, h, :], [f'Sf{h}'], [f'Sb{h}'])
                S.add('dve', lambda e: e.memset(ssq, 0.0), [], ['ssq'])
                for h in range(NH):
                    hs = slice(h * 128, (h + 1) * 128)
                    act(junk2, pso[:, hs], AF.Square, ['pso'], ['junk2', 'ssq'], accum=ssq[:, h:h + 1])
                act(rso, ssq, AF.Ln, ['ssq', 'epsc'], ['rso'], bias=eps_c, scale=1.0 / 128)
                act(rso, rso, AF.Exp, ['rso'], ['rso'], scale=-0.5)
                for h in range(NH):
                    hs = slice(h * 128, (h + 1) * 128)
                    stt('dve', bft['ghg'][:, hs], pso[:, hs], rso[:, h:h + 1], f32t['gz2'][:, hs], ALU.mult, ALU.mult,
                        ['pso', 'rso', 'gz2'], ['ghg'])
                pgt = P[6][:, 128:256].bitcast(BF16)
                for h in range(NH):
                    tr(pgt[:, h * 128:(h + 1) * 128], bft['ghg'][:, h * 128:(h + 1) * 128], ['ghg'], ['pgt'])
                cp('act', trT['ghgT'][:, :, :], pgt.rearrange("p (a b) -> p a b", b=128), ['pgt'], ['ghgT'])
                for h in range(NH):
                    fb = 4 + p * NH + h
                    if SUB < 2:
                        dma('sp', ibs[fb][:, Tt * 128:(Tt + 1) * 128], trT['ghgT'][:, h, :], ['ghgT'], [f'ib{fb}'])
            for h in range(NH):
                fb = 4 + p * NH + h
                if SUB >= 1:
                    continue
                S.add('pool', lambda e, fb=fb: e.collective_compute(
                    "AllGather", ALU.bypass, replica_groups=[[0, 1], [2, 3], [4, 5], [6, 7]],
                    ins=[ibs[fb].ap().opt()], outs=[obs[fb].ap().opt()]), [f'ib{fb}'], [f'ob{fb}'], kind='cc', g=False)

        if STOP < 3:
            S.off = True
        phase_barrier()
        cv = Carve()
        qnT = cv.get([NH, T], BF16)
        knT = cv.get([NH, T], BF16)
        Vaug = cv.get([NT, NH, 130], BF16)
        Zs = cv.get([NT, WD], BF16)
        ring = [cv.get([640], BF16) for _ in range(6)]
        EB = cv.get([NH, 640], BF16)
        bstage = cv.get([640], F32)
        qraw = cv.get([512], F32)
        sqb = cv.get([512], BF16)
        rq = cv.get([512], F32)
        tzz = cv.get([WD], F32)
        stageA = cv.get([128], BF16)
        GTh = cv.get([T], BF16)
        rc = cv.get([2], F32)
        maskf = cf[:, CF_MASK:CF_MASK + 640]
        for p in range(NPASS):
            for s in range(4):
                load_slab(s, s, p)
            S.add('dve', lambda e: e.memset(Vaug[:, :, :, 128:130], 1.0), [], ['Vaug'])
            for h in range(NH):
                hg = p * NH + h
                dma('sp', bstage, bias_d[:, hg * 640:(hg + 1) * 640], [], ['bstage'])
                act(bstage, bstage, AF.Exp, ['bstage'], ['bstage'])
                tt('dve', EB[:, h, :], bstage, maskf, ALU.mult, ['bstage', 'cf'], ['EB'])
            for Tt in range(NT):
                tok = slice(Tt * 128, (Tt + 1) * 128)
                pb = P[Tt % 2]
                pk = f'P{Tt % 2}'
                for c in range(NCH):
                    mm(pb[:, 0:WD], T_h[:, c, tok], T_w[2][:, c, :], c == 0, c == NCH - 1, [f'h{c}', 'Tw2'], [pk + 'v'])
                for c in range(NCH):
                    mm(pb[:, 256:256 + WD], T_h[:, c, tok], T_w[3][:, c, :], c == 0, c == NCH - 1, [f'h{c}', 'Tw3'], [pk + 'z'])
                cp('act', Vaug[:, Tt, :, 0:128], pb[:, 0:WD].rearrange("p (a b) -> p a b", b=128), [pk + 'v'], ['Vaug'])
                sigm(tzz, pb[:, 256:256 + WD], 'tzz', tzz, pk + 'z', 'tzz')
                tt('dve', Zs[:, Tt, :], pb[:, 256:256 + WD], tzz, ALU.mult, [pk + 'z', 'tzz'], ['Zs'])
            for h in range(NH):
                for (slot, dstT, gcol, nmk) in ((0, qnT, small[:, 0:1], 'qnT'), (1, knT, small[:, 1:2], 'knT')):
                    for nb in range(4):
                        tb = slice(nb * 512, (nb + 1) * 512)
                        pq = P[2 + (nb % 2)]
                        pqk = f'P{2 + (nb % 2)}'
                        for c in range(NCH):
                            mm(pq[:, :], T_w[slot][:, c, h * 128:(h + 1) * 128], T_h[:, c, tb], c == 0, c == NCH - 1,
                               [f'h{c}', f'Tw{slot}'], [pqk])
                        cp('act', qraw, pq[:, :], [pqk], ['qraw'])
                        act(sqb, pq[:, :], AF.Square, [pqk], ['sqb'])
                        mm(P[4][:, :], ones_bf, sqb, True, True, ['sqb', 'cbf'], ['P4'])
                        act(rq, P[4][:, :], AF.Ln, ['P4', 'epsc'], ['rq'], bias=eps_c, scale=1.0 / 128)
                        act(rq, rq, AF.Exp, ['rq'], ['rq'], scale=-0.5)
                        stt('dve', dstT[:, h, tb], qraw, gcol, rq, ALU.mult, ALU.mult, ['qraw', 'rq', 'gq', 'gk'], [nmk + str(h)])
            for h in range(NH):
                fb = p * NH + h
                for j in range(NT):
                    W = min(640, T - 128 * j)
                    W0 = min(W, 512)
                    rj = ring[j % 6]
                    rk = f'ring{j % 6}'
                    mm(P[5][:, 0:W0], knT[:, h, j * 128:(j + 1) * 128], qnT[:, h, j * 128:j * 128 + W0], True, True,
                       [f'knT{h}', f'qnT{h}'], ['P5'])
                    act(rj[:, 0:W0], P[5][:, 0:W0], AF.Exp, ['P5'], [rk])
                    if W > 512:
                        mm(P[6][:, 0:128], knT[:, h, j * 128:(j + 1) * 128], qnT[:, h, j * 128 + 512:j * 128 + 640], True, True,
                           [f'knT{h}', f'qnT{h}'], ['P6'])
                        act(rj[:, 512:640], P[6][:, 0:128], AF.Exp, ['P6'], [rk])
                    tt('dve', rj[:, 0:W], rj[:, 0:W], EB[:, h, 0:W], ALU.mult, [rk, 'EB'], [rk])
                    po = P[7][:, 0:129] if j % 2 == 0 else P[7][:, 256:385]
                    pok = f'po{j % 2}'
                    t0 = max(0, j - 4)
                    for t in range(t0, j + 1):
                        mm(po, ring[t % 6][:, (j - t) * 128:(j - t + 1) * 128], Vaug[:, t, h, 0:129], t == t0, t == j,
                           [f'ring{t % 6}', 'Vaug'], [pok])
                    rcp(rc[:, 0:1], po[:, 128:129], [pok], ['rc'])
                    stt('dve', stageA, po[:, 0:128], rc[:, 0:1], Zs[:, j, h * 128:(h + 1) * 128], ALU.mult, ALU.mult,
                        [pok, 'rc', 'Zs'], ['stageA'])
                    pst = P[6][:, 256:320].bitcast(BF16)
                    tr(pst, stageA, ['stageA'], ['pst'])
                    cp('act', GTh[:, j * 128:(j + 1) * 128], pst, ['pst'], ['GTh'])
                dma('sp', ibs[fb][:, :], GTh, ['GTh'], [f'ib{fb}'])
                S.add('pool', lambda e, fb=fb: e.collective_compute(
                    "AllGather", ALU.bypass, replica_groups=[[0, 1], [2, 3], [4, 5], [6, 7]],
                    ins=[ibs[fb].ap().opt()], outs=[obs[fb].ap().opt()]), [f'ib{fb}'], [f'ob{fb}'], kind='cc', g=False)

        if STOP < 4:
            S.off = True
        phase_barrier()
        cv = Carve()
        xr = [cv.get([1024], F32) for _ in range(2)]
        ot = [cv.get([512], F32) for _ in range(2)]
        for fb in range(8):
            for r_ in range(2):
                ch = fb * 2 + r_
                dma('sp', T_h[:, ch, :], obs[fb][r_ * 128:(r_ + 1) * 128, :], [f'ob{fb}'], [f'h{ch}'])
        wo_v = w_out_c.rearrange("(c p) n -> p c n", p=128)
        for q_ in range(4):
            for q4 in range(4):
                dma('pool', T_w[q_][:, q4 * 4:(q4 + 1) * 4, :], wo_v[:, q4 * 4:(q4 + 1) * 4, q_ * 256:(q_ + 1) * 256], [], [f'Tw{q_}'], g=False)
        k_ = 0
        for Tt in range(NT):
            tok = slice(Tt * 128, (Tt + 1) * 128)
            xrb = xr[Tt % 2]
            xrk = f'xr{Tt % 2}'
            dma('sp', xrb, x_res[tok, :], [], [xrk])
            for cb in range(2):
                pb = P[k_ % 4]
                pk = f'P{k_ % 4}'
                ob_ = ot[k_ % 2]
                ok_ = f'ot{k_ % 2}'
                k_ += 1
                for q2 in range(2):
                    q_ = cb * 2 + q2
                    for ch in range(NCH):
                        mm(pb[:, q2 * 256:(q2 + 1) * 256], T_h[:, ch, tok], T_w[q_][:, ch, :], ch == 0, ch == NCH - 1,
                           [f'h{ch}', f'Tw{q_}'], [pk])
                tt('dve', ob_, pb[:, :], gate_bc[:, cb * 512:(cb + 1) * 512], ALU.mult, [pk, 'gate'], [ok_])
                tt('dve', ob_, ob_, xrb[:, cb * 512:(cb + 1) * 512], ALU.add, [ok_, xrk], [ok_])
                dma('sp', out_d[tok, cb * 512:(cb + 1) * 512], ob_, [ok_], ['outd'])

        S.emit(block, sems, dma_sems, cc_sems)
    return nc


_NC_CACHE = {}


def _consts():
    idx = np.arange(128)
    ch = idx // 64
    m = idx % 64
    s_ = idx[:, None]
    t_ = idx[None, :]
    same = (ch[:, None] == ch[None, :])
    ms = m[:, None]
    mt = m[None, :]
    L_inter = same & (s_ <= t_)
    L_kdec = same & (s_ > t_)
    L_q1 = same & (mt >= 32) & (ms >= 32) & (ms <= mt)
    L_k1 = same & (mt < 32) & (ms > mt) & (ms <= 31)
    blk = idx // 32
    sameb = blk[:, None] == blk[None, :]
    L_q2 = sameb & (s_ <= t_)
    cb = np.zeros((128, NB), np.float32)
    cb[:, CB_ID:CB_ID + 128] = np.eye(128)
    cb[:, CB_ONE:CB_ONE + 128] = 1.0
    for i, L in enumerate((L_inter, L_kdec, L_q1, L_k1, L_q2)):
        cb[:, CB_L + i * 128:CB_L + (i + 1) * 128] = L.astype(np.float32)
    cb[:, CB_IND] = (ch == 0)
    cb[:, CB_IND + 1] = (ch == 1)
    M1T = (same & (ms < 32) & (mt >= 32)).astype(np.float32)
    M2T = (sameb & (s_ <= t_)).astype(np.float32)
    rbq1 = np.where(m >= 32, 0.0, NEG).astype(np.float32)
    rbk1 = np.where(m < 32, 0.0, NEG).astype(np.float32)
    kk = np.arange(128)[:, None]
    qq = np.arange(640)[None, :]
    dchunk = qq // 64 - kk // 64
    mask01 = ((dchunk >= 0) & (dchunk <= 8)).astype(np.float32)
    bidx = np.clip(qq - kk, -128, 128) + 128
    return cb, M1T, M2T, rbq1, rbk1, mask01, bidx


def kernel(x, c, norm_g, w_ada, b_ada, w_in, q_norm_g, k_norm_g, rel_bias, lower_bounds, hg_norm_g, w_out):
    x = np.asarray(x, np.float32)
    c = np.asarray(c, np.float32)
    w_ada = np.asarray(w_ada, np.float32)[0]
    b_ada = np.asarray(b_ada, np.float32)[0]
    w_in = np.asarray(w_in, np.float32)[0]
    w_out = np.asarray(w_out, np.float32)[0]
    norm_g = np.asarray(norm_g, np.float32)[0]
    q_norm_g = np.asarray(q_norm_g, np.float32)[0]
    k_norm_g = np.asarray(k_norm_g, np.float32)[0]
    rel_bias = np.asarray(rel_bias, np.float32)[0]
    lower_bounds = np.asarray(lower_bounds, np.float32)
    hg_norm_g = np.asarray(hg_norm_g, np.float32)[0]

    cb, M1T, M2T, rbq1, rbk1, mask01, bidx = _consts()
    if 'nc' not in _NC_CACHE:
        _NC_CACHE['nc'] = build_program()
    nc = _NC_CACHE['nc']

    in_maps = []
    for core in range(8):
        b, hh = core // 2, core % 2
        cfa = np.zeros((128, NF), np.float32)
        cfa[:, CF_M1:CF_M1 + WD] = np.tile(M1T, (1, NH))
        cfa[:, CF_M2:CF_M2 + WD] = np.tile(M2T, (1, NH))
        cfa[:, CF_RB] = rbq1
        cfa[:, CF_RB + 1] = rbk1
        cfa[:, CF_MASK:CF_MASK + 640] = mask01
        cfa[:, CF_OG:CF_OG + 512] = np.tile(hg_norm_g[None, :], (128, 4))
        cfa[:, CF_LB0:CF_LB0 + 512] = np.tile(lower_bounds[0, hh * 512:(hh + 1) * 512][None, :], (128, 1))
        cfa[:, CF_LB1:CF_LB1 + 512] = np.tile(lower_bounds[1, hh * 512:(hh + 1) * 512][None, :], (128, 1))
        cfa[:, CF_NG:CF_NG + 16] = norm_g.reshape(16, 128).T
        cfa[:, CF_GQ] = q_norm_g
        cfa[:, CF_GK] = k_norm_g
        cfa[:, CF_C:CF_C + 16] = c[b].reshape(16, 128).T
        cfa[:, CF_ONE:CF_ONE + 128] = 1.0
        cols = np.concatenate([np.arange(0, 4096), 4096 + hh * 1024 + np.arange(1024)])
        w_ada_c = np.ascontiguousarray(w_ada[:, cols])
        b_ada_c = np.ascontiguousarray(b_ada[cols][None, :])
        w_in_c = np.stack([w_in[:, s * 1024 + hh * 512: s * 1024 + (hh + 1) * 512] for s in range(8)], axis=0)
        rows = []
        for fb in range(8):
            for r_ in range(2):
                if fb < 4:
                    g = 4 * r_ + fb
                    rows.append(np.arange(g * 128, (g + 1) * 128))
                else:
                    g = 4 * r_ + (fb - 4)
                    rows.append(1024 + np.arange(g * 128, (g + 1) * 128))
        rows = np.concatenate(rows)
        w_out_c = np.ascontiguousarray(w_out[rows][:, hh * 1024:(hh + 1) * 1024])
        biasT = np.concatenate([rel_bias[4 * hh + h][bidx] for h in range(4)], axis=1).astype(np.float32)
        in_maps.append({
            "x_b": np.ascontiguousarray(x[b]),
            "x_res": np.ascontiguousarray(x[b][:, hh * 1024:(hh + 1) * 1024]),
            "w_ada_c": w_ada_c,
            "b_ada_c": b_ada_c,
            "w_in_c": np.ascontiguousarray(w_in_c),
            "w_out_c": w_out_c,
            "cbf": cb,
            "cf": cfa,
            "biasT": np.ascontiguousarray(biasT),
        })
    res = run_bass_kernel_spmd(nc, in_maps, core_ids=list(range(8)))
    out = np.zeros((4, T, D), np.float32)
    for core in range(8):
        b, hh = core // 2, core % 2
        out[b][:, hh * 1024:(hh + 1) * 1024] = res.results[core]["out"]
    return out
```

```python
import numpy as np
from contextlib import ExitStack
import concourse.bass as bass
import concourse.mybir as mybir
from concourse.bass_utils import run_bass_kernel_spmd

F32 = mybir.dt.float32
BF16 = mybir.dt.bfloat16
AF = mybir.ActivationFunctionType
ALU = mybir.AluOpType

D = 2048
T = 2048
NT = 16
NCH = 16
NH = 2
WD = NH * 128
NPASS = 2
EPS = 1e-6
NEG = -30000.0

CF_M1 = 0
CF_M2 = CF_M1 + WD
CF_RB = CF_M2 + WD
CF_MASK = CF_RB + 2
CF_OG = CF_MASK + 640
CF_LB0 = CF_OG + 512
CF_LB1 = CF_LB0 + 512
CF_NG = CF_LB1 + 512
CF_GQ = CF_NG + 16
CF_GK = CF_GQ + 1
CF_C = CF_GK + 1
CF_ONE = CF_C + 16
NF = CF_ONE + 128
CB_ID = 0
CB_ONE = 128
CB_L = 256
CB_IND = CB_L + 5 * 128
NB = CB_IND + 2 + 2


class Sched:
    def __init__(self):
        self.ops = []
        self.lastw = {}
        self.readers = {}
        self.gflag = False
        self.off = False
        self.budget = None

    BANK = {'psq': 'P0', 'psf': 'P0', 'psi': 'P1', 'psz': 'P1', 'pe_inter': 'P2', 'pe_kdec': 'P2',
            'pe_q1': 'P3', 'pe_k1': 'P3', 'pe_q2': 'P4', 'psbl': 'P4', 'pt_qi': 'P5', 'pt_q1': 'P5',
            'pt_k1': 'P5', 'pt_q2': 'P6', 'pt_k2': 'P4', 'pgt': 'P6', 'psA1': 'P6', 'pst': 'P6',
            'psA2': 'P7', 'pso': 'P7', 'po0': 'P7', 'po1': 'P7', 'P0v': 'P0', 'P0z': 'P0',
            'P1v': 'P1', 'P1z': 'P1'}

    def add(self, eng, fn, r=(), w=(), kind='c', g=None):
        if self.off:
            return -1
        if self.budget is not None:
            if self.budget <= 0:
                return -1
            self.budget -= 1
        i = len(self.ops)
        r = [self.BANK.get(k, k) for k in r]
        w = [self.BANK.get(k, k) for k in w]
        if (self.gflag if g is None else g):
            r.append('GPH')
        deps = set()
        for k in r:
            lw = self.lastw.get(k)
            if lw is not None:
                deps.add(lw)
        for k in w:
            lw = self.lastw.get(k)
            if lw is not None:
                deps.add(lw)
            rd = self.readers.get(k)
            if rd:
                deps.update(rd[0].values())
                deps.update(rd[1])
        deps.discard(i)
        self.ops.append((eng, fn, deps, kind))
        for k in r:
            rd = self.readers.setdefault(k, ({}, []))
            if kind == 'c':
                rd[0][eng] = i
            else:
                rd[1].append(i)
        for k in w:
            self.lastw[k] = i
            self.readers[k] = ({}, [])
        return i

    def barrier(self):
        pass

    def emit(self, block, sems, dma_sems, cc_sems):
        ops = self.ops
        n = len(ops)
        need = [False] * n
        for i, (eng, fn, deps, kind) in enumerate(ops):
            for d in deps:
                de, _, _, dk = ops[d]
                if de == 'pe' and eng == 'pe' and dk == 'c' and kind == 'c':
                    continue
                need[d] = True
        sig = {}
        pre = {}
        cnt = {}
        rr = 0
        rrp = 0
        dcnt = [0] * len(dma_sems)
        cci = 0
        for i, (eng, fn, deps, kind) in enumerate(ops):
            if kind == 'dma':
                if eng == 'pool':
                    k = 16 + (rrp % 8)
                    rrp += 1
                else:
                    k = rr % 16
                    rr += 1
                if dcnt[k] > 0:
                    pre[i] = (dma_sems[k], 16 * dcnt[k])
                dcnt[k] += 1
                sig[i] = (dma_sems[k], 16 * dcnt[k])
            elif kind == 'cc':
                sig[i] = (cc_sems[cci], 1)
                cci += 1
            elif need[i]:
                cnt[eng] = cnt.get(eng, 0) + 1
                sig[i] = (sems[eng], cnt[eng])
        order = {}
        for i, o in enumerate(ops):
            order.setdefault(o[0], []).append(i)

        def run(engname, e):
            waited = {}
            for i in order.get(engname, []):
                eng, fn, deps, kind = ops[i]
                wl = {}
                for d in deps:
                    if d not in sig:
                        continue
                    de, _, _, dk = ops[d]
                    if de == 'pe' and eng == 'pe' and dk == 'c' and kind == 'c':
                        continue
                    sm, val = sig[d]
                    key = id(sm)
                    if waited.get(key, 0) < val:
                        if key not in wl or wl[key][1] < val:
                            wl[key] = (sm, val)
                if i in pre:
                    sm, val = pre[i]
                    key = id(sm)
                    if waited.get(key, 0) < val:
                        if key not in wl or wl[key][1] < val:
                            wl[key] = (sm, val)
                for key, (sm, val) in wl.items():
                    e.wait_ge(sm, val)
                    waited[key] = val
                ins = fn(e)
                if i in sig:
                    sm, val = sig[i]
                    if kind == 'dma':
                        ins.then_inc(sm, 16)
                    elif kind == 'cc':
                        ins.then_inc(sm)
                    else:
                        ins.then_inc(sm, 1)
            if engname == 'sp':
                for k, sm in enumerate(dma_sems):
                    if dcnt[k] > 0:
                        e.wait_ge(sm, 16 * dcnt[k])
                for k in range(cci):
                    e.wait_ge(cc_sems[k], 1)

        @block.tensor
        def _(e):
            run('pe', e)

        @block.scalar
        def _(e):
            run('act', e)

        @block.vector
        def _(e):
            run('dve', e)

        @block.gpsimd
        def _(e):
            run('pool', e)

        @block.sync
        def _(e):
            run('sp', e)


DEBUG = False
STOP = 99
SUB = 0
CUT = -1


def build_program():
    nc = bass.Bass("TRN2", target_bir_lowering=False)
    S = Sched()
    dbg_h = nc.dram_tensor("dbg_h", [128, NCH, T], BF16) if DEBUG else None

    def din(name, shape):
        return nc.dram_tensor(name, shape, F32, kind="ExternalInput").ap()

    x_b = din("x_b", [T, D])
    x_res = din("x_res", [T, 1024])
    w_ada_c = din("w_ada_c", [D, 5120])
    b_ada_c = din("b_ada_c", [1, 5120])
    w_in_c = din("w_in_c", [8, D, 512])
    w_out_c = din("w_out_c", [D, 1024])
    cbf_d = din("cbf", [128, NB])
    cf_d = din("cf", [128, NF])
    bias_d = din("biasT", [128, 4 * 640])
    out_d = nc.dram_tensor("out", [T, 1024], F32, kind="ExternalOutput").ap()
    ibs = [nc.dram_tensor(f"ib{i}", [128, T], BF16) for i in range(8)]
    obs = [nc.dram_tensor(f"ob{i}", [256, T], BF16) for i in range(8)]

    es = ExitStack()
    with es:
        def sb(name, shape, dt):
            return es.enter_context(nc.sbuf_tensor(name, shape, dt))

        def ps(name):
            return es.enter_context(nc.psum_tensor(name, [128, 512], F32))

        T_h = sb("T_h", [128, NCH, T], BF16)
        T_w = [sb(f"T_w{i}", [128, NCH, WD], BF16) for i in range(4)]
        cbf = sb("cbf_s", [128, NB], BF16)
        cf = sb("cf_s", [128, NF], F32)
        GSZ = 31 * 1024
        G = sb("G", [128, GSZ], BF16)
        modT = sb("modT", [128, 32], F32)
        sc1 = sb("sc1", [128, 16], F32)
        gate_bc = sb("gate_bc", [128, 1024], F32)
        cact = sb("cact", [128, 16], BF16)
        small = sb("small", [128, 64], F32)
        lb_bc = sb("lb_bc", [128, 512], F32)
        oml_bc = sb("oml_bc", [128, 512], F32)
        rowblk = sb("rowblk", [1, 512], F32)
        bblk = sb("bblk", [1, 512], F32)
        scratch = sb("scratch", [128, 4], F32)
        P = [ps(f"P{i}") for i in range(8)]

        sems = {k: es.enter_context(nc.semaphore("s_" + k)) for k in ('pe', 'act', 'dve', 'pool', 'sp')}
        dma_sems = [es.enter_context(nc.semaphore(f"dq{i}")) for i in range(24)]
        cc_sems = [es.enter_context(nc.semaphore(f"cc{i}")) for i in range(8)]
        block = es.enter_context(nc.Block())

        ident = cbf[:, CB_ID:CB_ID + 128]
        ones_bf = cbf[:, CB_ONE:CB_ONE + 128]

        def Lm(i):
            return cbf[:, CB_L + i * 128: CB_L + (i + 1) * 128]
        ind2 = cbf[:, CB_IND:CB_IND + 2]

        def mm(out, lhsT, rhs, start, stop, r, w):
            S.add('pe', lambda e: e.matmul(out, lhsT, rhs, start=start, stop=stop), r, w)

        def tr(out, in_, r, w):
            S.add('pe', lambda e: e.transpose(out, in_, ident), list(r) + ['cbf'], w)

        def act(out, in_, func, r, w, bias=None, scale=None, accum=None):
            kw = {}
            if bias is not None:
                kw['bias'] = bias
            if scale is not None:
                kw['scale'] = scale
            if accum is not None:
                kw['accum_out'] = accum
            S.add('act', lambda e: e.activation(out, in_, func, **kw), r, w)

        def tt(eng, out, a, b, op, r, w):
            S.add(eng, lambda e: e.tensor_tensor(out, a, b, op), r, w)

        def ts(eng, out, a, s1, s2, op0, op1, r, w):
            if op1 is None:
                S.add(eng, lambda e: e.tensor_scalar(out, a, s1, None, op0=op0), r, w)
            else:
                S.add(eng, lambda e: e.tensor_scalar(out, a, s1, s2, op0=op0, op1=op1), r, w)

        def stt(eng, out, a, sc, b, op0, op1, r, w):
            S.add(eng, lambda e: e.scalar_tensor_tensor(out, a, sc, b, op0=op0, op1=op1), r, w)

        def cp(eng, out, in_, r, w):
            if eng == 'act':
                S.add('act', lambda e: e.copy(out, in_), r, w)
            else:
                S.add(eng, lambda e: e.tensor_copy(out, in_), r, w)

        def rcp(out, in_, r, w):
            S.add('dve', lambda e: e.reciprocal(out, in_), r, w)

        def dma(q, out, in_, r, w, g=None):
            S.add(q, lambda e: e.dma_start(out=out, in_=in_), r, w, kind='dma', g=g)

        def phase_barrier():
            S.add('dve', lambda e: e.memset(scratch[:, 0:1], 0.0), r=(), w=('GPH',), g=False)

        class Carve:
            def __init__(self):
                self.off = 0

            def get(self, free, dt):
                n = 1
                for f in free:
                    n *= f
                nb = n * (2 if dt == BF16 else 4)
                nb = (nb + 3) // 4 * 4
                a = self.off // 2
                assert a + nb // 2 <= GSZ, (a, nb, GSZ)
                v = G[:, a:a + nb // 2]
                self.off += nb
                if dt == F32:
                    v = v.bitcast(F32)
                if len(free) == 2:
                    v = v.rearrange("p (a b) -> p a b", b=free[1])
                elif len(free) == 3:
                    v = v.rearrange("p (a b c) -> p a b c", b=free[1], c=free[2])
                return v

        dma('pool', cbf[:], cbf_d[:, :], [], ['cbf'], g=False)
        dma('sp', cf[:], cf_d[:, :], [], ['cf'], g=False)

        tt('dve', lb_bc[:], cf[:, CF_LB0:CF_LB0 + 512], cf[:, CF_LB1:CF_LB1 + 512], ALU.subtract, ['cf'], ['lb'])
        act(lb_bc[:], lb_bc[:], AF.Exp, ['lb'], ['lb'], scale=-1.0)
        ts('dve', lb_bc[:], lb_bc[:], 1.0, None, ALU.add, None, ['lb'], ['lb'])
        rcp(lb_bc[:], lb_bc[:], ['lb'], ['lb'])
        ts('dve', oml_bc[:], lb_bc[:], -1.0, 1.0, ALU.mult, ALU.add, ['lb'], ['oml'])
        ts('dve', small[:, 0:1], cf[:, CF_GQ:CF_GQ + 1], 128.0 ** -0.5, None, ALU.mult, None, ['cf'], ['gq'])
        cp('dve', small[:, 1:2], cf[:, CF_GK:CF_GK + 1], ['cf'], ['gk'])
        ts('dve', small[:, 2:3], cf[:, CF_GK:CF_GK + 1], 0.0, EPS, ALU.mult, ALU.add, ['cf'], ['epsc'])
        eps_c = small[:, 2:3]

        cT = cf[:, CF_C:CF_C + 16]
        act(small[:, 16:32], cT, AF.Exp, ['cf'], ['ctmp'], scale=-1.0)
        ts('dve', small[:, 16:32], small[:, 16:32], 1.0, None, ALU.add, None, ['ctmp'], ['ctmp'])
        rcp(small[:, 16:32], small[:, 16:32], ['ctmp'], ['ctmp'])
        tt('dve', cact[:], small[:, 16:32], cT, ALU.mult, ['ctmp', 'cf'], ['cact'])
        ones_f = cf[:, CF_ONE:CF_ONE + 128]
        wada_v = w_ada_c.rearrange("(c p) n -> p c n", p=128)
        for blk in range(20):
            wb = T_w[blk % 4]
            wk = f'Tw{blk % 4}'
            for q4 in range(4):
                dma('pool', wb[:, q4 * 4:(q4 + 1) * 4, :], wada_v[:, q4 * 4:(q4 + 1) * 4, blk * 256:(blk + 1) * 256], [], [wk], g=False)
            dma('sp', bblk[0:1, 0:256], b_ada_c[0:1, blk * 256:(blk + 1) * 256], [], ['bblk'], g=False)
            for c in range(NCH):
                mm(P[0][0:1, 0:256], cact[:, c:c + 1], wb[:, c, :], c == 0, c == NCH - 1, [wk, 'cact'], ['P0'])
            tt('dve', rowblk[0:1, 0:256], P[0][0:1, 0:256], bblk[0:1, 0:256], ALU.add, ['P0', 'bblk'], ['rowblk'])
            if blk < 16:
                for s in range(2):
                    col = blk * 2 + s
                    mm(P[1][:, col:col + 1], rowblk[0:1, s * 128:(s + 1) * 128], ones_f[0:1, 0:1], True, True,
                       ['rowblk', 'cf'], ['P1'])
            else:
                gb = blk - 16
                mm(P[2][:, 0:256], ones_f[0:1, 0:128], rowblk[0:1, 0:256], True, True, ['rowblk', 'cf'], ['P2'])
                cp('dve', gate_bc[:, gb * 256:(gb + 1) * 256], P[2][:, 0:256], ['P2'], ['gate'])
            if blk == 15:
                cp('dve', modT[:], P[1][:, 0:32], ['P1'], ['modT'])
        stt('dve', sc1[:], modT[:, 16:32], 1.0, cf[:, CF_NG:CF_NG + 16], ALU.add, ALU.mult, ['modT', 'cf'], ['sc1'])

        if STOP < 1:
            S.off = True
        S.gflag = True
        phase_barrier()
        cv = Carve()
        xs = [cv.get([D], F32) for _ in range(2)]
        xn = cv.get([4, D], BF16)
        junk = cv.get([D], BF16)
        ssum = small[:, 32:48]
        S.add('dve', lambda e: e.memset(ssum, 0.0), [], [f'ss{i}' for i in range(16)])
        rstd = small[:, 48:64]
        for gI in range(4):
            for ti in range(4):
                Tt = gI * 4 + ti
                xb_ = xs[Tt % 2]
                xk = f'xs{Tt % 2}'
                dma('sp', xb_, x_b[Tt * 128:(Tt + 1) * 128, :], [], [xk])
                act(junk, xb_, AF.Square, [xk], ['junk', f'ss{Tt}'], accum=ssum[:, Tt:Tt + 1])
                act(rstd[:, Tt:Tt + 1], ssum[:, Tt:Tt + 1], AF.Ln, [f'ss{Tt}', 'epsc'], [f'rs{Tt}'], bias=eps_c, scale=1.0 / D)
                act(rstd[:, Tt:Tt + 1], rstd[:, Tt:Tt + 1], AF.Exp, [f'rs{Tt}'], [f'rs{Tt}'], scale=-0.5)
                ts('dve', xn[:, ti, :], xb_, rstd[:, Tt:Tt + 1], None, ALU.mult, None, [xk, f'rs{Tt}'], [f'xn{ti}'])
            for c in range(NCH):
                pb = P[c % 4]
                pk = f'P{c % 4}'
                pbv = pb[:, 0:256].bitcast(BF16)
                for ti in range(4):
                    tr(pbv[:, ti * 128:(ti + 1) * 128], xn[:, ti, c * 128:(c + 1) * 128], [f'xn{ti}'], [pk])
                dst = T_h[:, c, gI * 512:(gI + 1) * 512]
                if c % 2 == 0:
                    act(dst, pbv, AF.Identity, [pk, 'sc1', 'modT'], [f'h{c}'], bias=modT[:, c:c + 1], scale=sc1[:, c:c + 1])
                else:
                    ts('dve', dst, pbv, sc1[:, c:c + 1], modT[:, c:c + 1], ALU.mult, ALU.add, [pk, 'sc1', 'modT'], [f'h{c}'])
        hkeys = [f'h{c}' for c in range(NCH)]
        if DEBUG:
            dma('sp', dbg_h[:, :, :], T_h[:, :, :], hkeys, ['dbgh'])

        def load_slab(slot, s_idx, p):
            src = w_in_c[s_idx].rearrange("(c p) n -> p c n", p=128)
            for q4 in range(4):
                dma('pool', T_w[slot][:, q4 * 4:(q4 + 1) * 4, :], src[:, q4 * 4:(q4 + 1) * 4, p * WD:(p + 1) * WD], [], [f'Tw{slot}'], g=False)

        if STOP < 2:
            S.off = True
        if CUT >= 0:
            S.budget = CUT
        phase_barrier()
        cv = Carve()
        f32t = {k: cv.get([WD], F32) for k in ('tq', 'qf', 'tf', 'ff', 'logf', 'kk', 'tz', 'gz2', 'E', 'A1', 'A2')}
        bft = {k: cv.get([WD], BF16) for k in ('hi', 'lo', 'vb', 'qi', 'kd', 'q1', 'k1', 'q2', 'k2', 'ghg')}
        ATb = cv.get([NH, 128], BF16)
        trT = {k: cv.get([NH, 128], BF16) for k in ('qiT0', 'qiT1', 'q1T', 'k1T', 'q2T', 'k2T', 'ghgT')}
        Sf = cv.get([NH, 128], F32)
        Sb = cv.get([NH, 128], BF16)
        Dd = cv.get([2 * NH], F32)
        ssq = cv.get([NH], F32)
        rso = cv.get([NH], F32)
        junk2 = cv.get([128], BF16)
        rbq1 = cf[:, CF_RB:CF_RB + 1]
        rbk1 = cf[:, CF_RB + 1:CF_RB + 2]
        M1 = cf[:, CF_M1:CF_M1 + WD]
        M2 = cf[:, CF_M2:CF_M2 + WD]

        def sigm(dst, src_ps, tmpk, tmp, pk, outk):
            act(tmp, src_ps, AF.Exp, [pk], [tmpk], scale=-1.0)
            ts('dve', tmp, tmp, 1.0, None, ALU.add, None, [tmpk], [tmpk])
            rcp(dst, tmp, [tmpk], [outk])

        for p in range(NPASS):
            for s in range(4):
                load_slab(s, 4 + s, p)
            S.add('dve', lambda e: e.memset(trT['qiT0'][:, :, :], 0.0), [], ['qiT0'])
            S.add('dve', lambda e: e.memset(trT['qiT1'][:, :, :], 0.0), [], ['qiT1'])
            S.add('dve', lambda e: e.memset(Sf[:, :, :], 0.0), [], ['Sf'])
            S.add('dve', lambda e: e.memset(Sb[:, :, :], 0.0), [], ['Sb'])
            lbp = lb_bc[:, p * WD:(p + 1) * WD]
            omlp = oml_bc[:, p * WD:(p + 1) * WD]
            ogp = cf[:, CF_OG + p * WD: CF_OG + (p + 1) * WD]
            NTL = NT if SUB < 3 else SUB - 2
            psq, psf = P[0][:, 0:WD], P[0][:, 256:256 + WD]
            psi, psz = P[1][:, 0:WD], P[1][:, 256:256 + WD]

            def hproj(Tn):
                tk = slice(Tn * 128, (Tn + 1) * 128)
                for (dst, dk, slot) in ((psq, 'psq', 0), (psf, 'psf', 1), (psi, 'psi', 2), (psz, 'psz', 3)):
                    for c in range(NCH):
                        mm(dst, T_h[:, c, tk], T_w[slot][:, c, :], c == 0, c == NCH - 1, [f'h{c}', f'Tw{slot}'], [dk])
            hproj(0)
            for Tt in range(NTL):
                tok = slice(Tt * 128, (Tt + 1) * 128)
                sigm(f32t['tq'], psq, 'tq', f32t['tq'], 'psq', 'tq')
                tt('dve', f32t['qf'], psq, f32t['tq'], ALU.mult, ['psq', 'tq'], ['qf'])
                sigm(f32t['tf'], psf, 'tf', f32t['tf'], 'psf', 'tf')
                tt('dve', f32t['ff'], f32t['tf'], omlp, ALU.mult, ['tf', 'oml'], ['ff'])
                tt('dve', f32t['ff'], f32t['ff'], lbp, ALU.add, ['ff', 'lb'], ['ff'])
                act(f32t['logf'], f32t['ff'], AF.Ln, ['ff'], ['logf'])
                ts('dve', f32t['kk'], f32t['ff'], -1.0, 1.0, ALU.mult, ALU.add, ['ff'], ['kk'])
                cp('dve', bft['hi'], f32t['logf'], ['logf'], ['hi'])
                tt('dve', bft['lo'], f32t['logf'], bft['hi'], ALU.subtract, ['logf', 'hi'], ['lo'])
                cp('act', bft['vb'], psi, ['psi'], ['vb'])
                sigm(f32t['tz'], psz, 'tz', f32t['tz'], 'psz', 'tz')
                tt('dve', f32t['gz2'], psz, f32t['tz'], ALU.mult, ['psz', 'tz'], ['gz2'])
                tt('dve', f32t['gz2'], f32t['gz2'], ogp, ALU.mult, ['gz2', 'cf'], ['gz2'])
                if Tt + 1 < NTL:
                    hproj(Tt + 1)
                pes = {'inter': (P[2][:, 0:WD], 0), 'kdec': (P[2][:, 256:256 + WD], 1), 'q1': (P[3][:, 0:WD], 2),
                       'k1': (P[3][:, 256:256 + WD], 3), 'q2': (P[4][:, 0:WD], 4)}
                for nm, (pt, li) in pes.items():
                    mm(pt, Lm(li), bft['hi'], True, False, ['hi', 'cbf'], ['pe_' + nm])
                    mm(pt, Lm(li), bft['lo'], False, True, ['lo', 'cbf'], ['pe_' + nm])
                psbl = P[4][:, 256:256 + 2 * NH]
                for h in range(NH):
                    mm(psbl[:, 2 * h:2 * h + 2], bft['hi'][:, h * 128:(h + 1) * 128], ind2, True, False, ['hi', 'cbf'], ['psbl'])
                    mm(psbl[:, 2 * h:2 * h + 2], bft['lo'][:, h * 128:(h + 1) * 128], ind2, False, True, ['lo', 'cbf'], ['psbl'])
                act(Dd, psbl, AF.Exp, ['psbl'], ['Dd'])

                def expmul(pe_nm, src_k, src, dst_k, bias=None, scale=None):
                    act(f32t['E'], pes[pe_nm][0], AF.Exp, ['pe_' + pe_nm, 'cf'], ['E'], bias=bias, scale=scale)
                    tt('dve', bft[dst_k], src, f32t['E'], ALU.mult, [src_k, 'E'], [dst_k])
                expmul('inter', 'qf', f32t['qf'], 'qi')
                expmul('kdec', 'kk', f32t['kk'], 'kd')
                expmul('q1', 'qf', f32t['qf'], 'q1', bias=rbq1)
                expmul('k1', 'kk', f32t['kk'], 'k1', bias=rbk1)
                expmul('q2', 'qf', f32t['qf'], 'q2')
                expmul('q2', 'kk', f32t['kk'], 'k2', scale=-1.0)
                P5b = P[5][:, :].bitcast(BF16)
                P6b = P[6][:, 0:128].bitcast(BF16)
                P4b = P[4][:, 384:512].bitcast(BF16)
                tslots = {'qi': P5b[:, 0:WD], 'q1': P5b[:, 256:256 + WD], 'k1': P5b[:, 512:512 + WD],
                          'q2': P6b[:, 0:WD], 'k2': P4b[:, 0:WD]}
                for nm, pt in tslots.items():
                    for h in range(NH):
                        tr(pt[:, h * 128:(h + 1) * 128], bft[nm][:, h * 128:(h + 1) * 128], [nm], ['pt_' + nm])
                ptq = tslots['qi'].rearrange("p (a b) -> p a b", b=128)
                cp('dve', trT['qiT0'][:, :, 0:64], ptq[:, :, 0:64], ['pt_qi'], ['qiT0'])
                cp('dve', trT['qiT1'][:, :, 64:128], ptq[:, :, 64:128], ['pt_qi'], ['qiT1'])
                for nm in ('q1', 'k1', 'q2', 'k2'):
                    eng = 'act' if nm in ('q1', 'q2') else 'dve'
                    cp(eng, trT[nm + 'T'][:, :, :], tslots[nm].rearrange("p (a b) -> p a b", b=128), ['pt_' + nm], [nm + 'T'])
                psA1 = P[6][:, 256:256 + WD]
                psA2 = P[7][:, 0:WD]
                pso = P[7][:, 256:256 + WD]
                for h in range(NH):
                    hs = slice(h * 128, (h + 1) * 128)
                    mm(psA1[:, hs], trT['k1T'][:, h, :], trT['q1T'][:, h, :], True, True, ['k1T', 'q1T'], ['psA1'])
                    mm(psA2[:, hs], trT['k2T'][:, h, :], trT['q2T'][:, h, :], True, True, ['k2T', 'q2T'], ['psA2'])
                tt('dve', f32t['A1'], psA1, M1, ALU.mult, ['psA1', 'cf'], ['A1'])
                tt('dve', f32t['A2'], psA2, M2, ALU.mult, ['psA2', 'cf'], ['A2'])
                tt('dve', ATb[:, :, :], f32t['A1'].rearrange("p (a b) -> p a b", b=128),
                   f32t['A2'].rearrange("p (a b) -> p a b", b=128), ALU.add, ['A1', 'A2'], ['ATb'])
                psU = [P[2][:, 0:WD], P[2][:, 256:256 + WD]]
                for h in range(NH):
                    hs = slice(h * 128, (h + 1) * 128)
                    mm(pso[:, hs], ATb[:, h, :], bft['vb'][:, hs], True, False, ['ATb', 'vb'], ['pso'])
                    mm(pso[:, hs], trT['qiT0'][:, h, :], Sb[:, h, :], False, False, ['qiT0', f'Sb{h}'], ['pso'])
                    mm(psU[0][:, hs], bft['kd'][0:64, hs], bft['vb'][0:64, hs], True, True, ['kd', 'vb'], ['pe_inter'])
                    stt('dve', Sf[:, h, :], Sf[:, h, :], Dd[:, 2 * h:2 * h + 1], psU[0][:, hs], ALU.mult, ALU.add,
                        ['pe_inter', 'Dd', f'Sf{h}'], [f'Sf{h}'])
                    cp('act', Sb[:, h, :], Sf[:, h, :], [f'Sf{h}'], [f'Sb{h}'])
                    mm(pso[:, hs], trT['qiT1'][:, h, :], Sb[:, h, :], False, True, ['qiT1', f'Sb{h}'], ['pso'])
                    mm(psU[1][:, hs], bft['kd'][64:128, hs], bft['vb'][64:128, hs], True, True, ['kd', 'vb'], ['pe_kdec'])
                    stt('dve', Sf[:, h, :], Sf[:, h, :], Dd[:, 2 * h + 1:2 * h + 2], psU[1][:, hs], ALU.mult, ALU.add,
                        ['pe_kdec', 'Dd', f'Sf{h}'], [f'Sf{h}'])
                    cp('act', Sb[:, h, :], Sf[:, h, :], [f'Sf{h}'], [f'Sb{h}'])
                S.add('dve', lambda e: e.memset(ssq, 0.0), [], ['ssq'])
                for h in range(NH):
                    hs = slice(h * 128, (h + 1) * 128)
                    act(junk2, pso[:, hs], AF.Square, ['pso'], ['junk2', 'ssq'], accum=ssq[:, h:h + 1])
                act(rso, ssq, AF.Ln, ['ssq', 'epsc'], ['rso'], bias=eps_c, scale=1.0 / 128)
                act(rso, rso, AF.Exp, ['rso'], ['rso'], scale=-0.5)
                for h in range(NH):
                    hs = slice(h * 128, (h + 1) * 128)
                    stt('dve', bft['ghg'][:, hs], pso[:, hs], rso[:, h:h + 1], f32t['gz2'][:, hs], ALU.mult, ALU.mult,
                        ['pso', 'rso', 'gz2'], ['ghg'])
                pgt = P[6][:, 128:256].bitcast(BF16)
                for h in range(NH):
                    tr(pgt[:, h * 128:(h + 1) * 128], bft['ghg'][:, h * 128:(h + 1) * 128], ['ghg'], ['pgt'])
                cp('act', trT['ghgT'][:, :, :], pgt.rearrange("p (a b) -> p a b", b=128), ['pgt'], ['ghgT'])
                for h in range(NH):
                    fb = 4 + p * NH + h
                    if SUB < 2:
                        dma('sp', ibs[fb][:, Tt * 128:(Tt + 1) * 128], trT['ghgT'][:, h, :], ['ghgT'], [f'ib{fb}'])
            for h in range(NH):
                fb = 4 + p * NH + h
                if SUB >= 1:
                    continue
                S.add('pool', lambda e, fb=fb: e.collective_compute(
                    "AllGather", ALU.bypass, replica_groups=[[0, 1], [2, 3], [4, 5], [6, 7]],
                    ins=[ibs[fb].ap().opt()], outs=[obs[fb].ap().opt()]), [f'ib{fb}'], [f'ob{fb}'], kind='cc', g=False)

        if STOP < 3:
            S.off = True
        phase_barrier()
        cv = Carve()
        qnT = cv.get([NH, T], BF16)
        knT = cv.get([NH, T], BF16)
        Vaug = cv.get([NT, NH, 130], BF16)
        Zs = cv.get([NT, WD], BF16)
        ring = [cv.get([640], BF16) for _ in range(6)]
        EB = cv.get([NH, 640], BF16)
        bstage = cv.get([640], F32)
        qraw = cv.get([512], F32)
        sqb = cv.get([512], BF16)
        rq = cv.get([512], F32)
        tzz = cv.get([WD], F32)
        stageA = cv.get([128], BF16)
        GTh = cv.get([T], BF16)
        rc = cv.get([2], F32)
        maskf = cf[:, CF_MASK:CF_MASK + 640]
        for p in range(NPASS):
            for s in range(4):
                load_slab(s, s, p)
            S.add('dve', lambda e: e.memset(Vaug[:, :, :, 128:130], 1.0), [], ['Vaug'])
            for h in range(NH):
                hg = p * NH + h
                dma('sp', bstage, bias_d[:, hg * 640:(hg + 1) * 640], [], ['bstage'])
                act(bstage, bstage, AF.Exp, ['bstage'], ['bstage'])
                tt('dve', EB[:, h, :], bstage, maskf, ALU.mult, ['bstage', 'cf'], ['EB'])
            for Tt in range(NT):
                tok = slice(Tt * 128, (Tt + 1) * 128)
                pb = P[Tt % 2]
                pk = f'P{Tt % 2}'
                for c in range(NCH):
                    mm(pb[:, 0:WD], T_h[:, c, tok], T_w[2][:, c, :], c == 0, c == NCH - 1, [f'h{c}', 'Tw2'], [pk + 'v'])
                for c in range(NCH):
                    mm(pb[:, 256:256 + WD], T_h[:, c, tok], T_w[3][:, c, :], c == 0, c == NCH - 1, [f'h{c}', 'Tw3'], [pk + 'z'])
                cp('act', Vaug[:, Tt, :, 0:128], pb[:, 0:WD].rearrange("p (a b) -> p a b", b=128), [pk + 'v'], ['Vaug'])
                sigm(tzz, pb[:, 256:256 + WD], 'tzz', tzz, pk + 'z', 'tzz')
                tt('dve', Zs[:, Tt, :], pb[:, 256:256 + WD], tzz, ALU.mult, [pk + 'z', 'tzz'], ['Zs'])
            for h in range(NH):
                for (slot, dstT, gcol, nmk) in ((0, qnT, small[:, 0:1], 'qnT'), (1, knT, small[:, 1:2], 'knT')):
                    for nb in range(4):
                        tb = slice(nb * 512, (nb + 1) * 512)
                        pq = P[2 + (nb % 2)]
                        pqk = f'P{2 + (nb % 2)}'
                        for c in range(NCH):
                            mm(pq[:, :], T_w[slot][:, c, h * 128:(h + 1) * 128], T_h[:, c, tb], c == 0, c == NCH - 1,
                               [f'h{c}', f'Tw{slot}'], [pqk])
                        cp('act', qraw, pq[:, :], [pqk], ['qraw'])
                        act(sqb, pq[:, :], AF.Square, [pqk], ['sqb'])
                        mm(P[4][:, :], ones_bf, sqb, True, True, ['sqb', 'cbf'], ['P4'])
                        act(rq, P[4][:, :], AF.Ln, ['P4', 'epsc'], ['rq'], bias=eps_c, scale=1.0 / 128)
                        act(rq, rq, AF.Exp, ['rq'], ['rq'], scale=-0.5)
                        stt('dve', dstT[:, h, tb], qraw, gcol, rq, ALU.mult, ALU.mult, ['qraw', 'rq', 'gq', 'gk'], [nmk + str(h)])
            for h in range(NH):
                fb = p * NH + h
                def smm(jn):
                    Wn = min(640, T - 128 * jn)
                    mm(P[5][:, 0:min(Wn, 512)], knT[:, h, jn * 128:(jn + 1) * 128], qnT[:, h, jn * 128:jn * 128 + min(Wn, 512)],
                       True, True, [f'knT{h}', f'qnT{h}'], ['P5'])
                    if Wn > 512:
                        mm(P[6][:, 0:128], knT[:, h, jn * 128:(jn + 1) * 128], qnT[:, h, jn * 128 + 512:jn * 128 + 640],
                           True, True, [f'knT{h}', f'qnT{h}'], ['P6'])
                smm(0)
                for j in range(NT):
                    W = min(640, T - 128 * j)
                    W0 = min(W, 512)
                    rj = ring[j % 6]
                    rk = f'ring{j % 6}'
                    act(rj[:, 0:W0], P[5][:, 0:W0], AF.Exp, ['P5'], [rk])
                    if W > 512:
                        act(rj[:, 512:640], P[6][:, 0:128], AF.Exp, ['P6'], [rk])
                    if j + 1 < NT:
                        smm(j + 1)
                    tt('dve', rj[:, 0:W], rj[:, 0:W], EB[:, h, 0:W], ALU.mult, [rk, 'EB'], [rk])
                    po = P[7][:, 0:129] if j % 2 == 0 else P[7][:, 256:385]
                    pok = f'po{j % 2}'
                    t0 = max(0, j - 4)
                    for t in range(t0, j + 1):
                        mm(po, ring[t % 6][:, (j - t) * 128:(j - t + 1) * 128], Vaug[:, t, h, 0:129], t == t0, t == j,
                           [f'ring{t % 6}', 'Vaug'], [pok])
                    rcp(rc[:, 0:1], po[:, 128:129], [pok], ['rc'])
                    stt('dve', stageA, po[:, 0:128], rc[:, 0:1], Zs[:, j, h * 128:(h + 1) * 128], ALU.mult, ALU.mult,
                        [pok, 'rc', 'Zs'], ['stageA'])
                    pst = P[6][:, 256:320].bitcast(BF16)
                    tr(pst, stageA, ['stageA'], ['pst'])
                    cp('act', GTh[:, j * 128:(j + 1) * 128], pst, ['pst'], ['GTh'])
                dma('sp', ibs[fb][:, :], GTh, ['GTh'], [f'ib{fb}'])
                S.add('pool', lambda e, fb=fb: e.collective_compute(
                    "AllGather", ALU.bypass, replica_groups=[[0, 1], [2, 3], [4, 5], [6, 7]],
                    ins=[ibs[fb].ap().opt()], outs=[obs[fb].ap().opt()]), [f'ib{fb}'], [f'ob{fb}'], kind='cc', g=False)

        if STOP < 4:
            S.off = True
        phase_barrier()
        cv = Carve()
        xr = [cv.get([1024], F32) for _ in range(2)]
        ot = [cv.get([512], F32) for _ in range(2)]
        for fb in range(8):
            for r_ in range(2):
                ch = fb * 2 + r_
                dma('sp', T_h[:, ch, :], obs[fb][r_ * 128:(r_ + 1) * 128, :], [f'ob{fb}'], [f'h{ch}'])
        wo_v = w_out_c.rearrange("(c p) n -> p c n", p=128)
        for q_ in range(4):
            for q4 in range(4):
                dma('pool', T_w[q_][:, q4 * 4:(q4 + 1) * 4, :], wo_v[:, q4 * 4:(q4 + 1) * 4, q_ * 256:(q_ + 1) * 256], [], [f'Tw{q_}'], g=False)
        k_ = 0
        for Tt in range(NT):
            tok = slice(Tt * 128, (Tt + 1) * 128)
            xrb = xr[Tt % 2]
            xrk = f'xr{Tt % 2}'
            dma('sp', xrb, x_res[tok, :], [], [xrk])
            for cb in range(2):
                pb = P[k_ % 4]
                pk = f'P{k_ % 4}'
                ob_ = ot[k_ % 2]
                ok_ = f'ot{k_ % 2}'
                k_ += 1
                for q2 in range(2):
                    q_ = cb * 2 + q2
                    for ch in range(NCH):
                        mm(pb[:, q2 * 256:(q2 + 1) * 256], T_h[:, ch, tok], T_w[q_][:, ch, :], ch == 0, ch == NCH - 1,
                           [f'h{ch}', f'Tw{q_}'], [pk])
                tt('dve', ob_, pb[:, :], gate_bc[:, cb * 512:(cb + 1) * 512], ALU.mult, [pk, 'gate'], [ok_])
                tt('dve', ob_, ob_, xrb[:, cb * 512:(cb + 1) * 512], ALU.add, [ok_, xrk], [ok_])
                dma('sp', out_d[tok, cb * 512:(cb + 1) * 512], ob_, [ok_], ['outd'])

        S.emit(block, sems, dma_sems, cc_sems)
    return nc


_NC_CACHE = {}


def _consts():
    idx = np.arange(128)
    ch = idx // 64
    m = idx % 64
    s_ = idx[:, None]
    t_ = idx[None, :]
    same = (ch[:, None] == ch[None, :])
    ms = m[:, None]
    mt = m[None, :]
    L_inter = same & (s_ <= t_)
    L_kdec = same & (s_ > t_)
    L_q1 = same & (mt >= 32) & (ms >= 32) & (ms <= mt)
    L_k1 = same & (mt < 32) & (ms > mt) & (ms <= 31)
    blk = idx // 32
    sameb = blk[:, None] == blk[None, :]
    L_q2 = sameb & (s_ <= t_)
    cb = np.zeros((128, NB), np.float32)
    cb[:, CB_ID:CB_ID + 128] = np.eye(128)
    cb[:, CB_ONE:CB_ONE + 128] = 1.0
    for i, L in enumerate((L_inter, L_kdec, L_q1, L_k1, L_q2)):
        cb[:, CB_L + i * 128:CB_L + (i + 1) * 128] = L.astype(np.float32)
    cb[:, CB_IND] = (ch == 0)
    cb[:, CB_IND + 1] = (ch == 1)
    M1T = (same & (ms < 32) & (mt >= 32)).astype(np.float32)
    M2T = (sameb & (s_ <= t_)).astype(np.float32)
    rbq1 = np.where(m >= 32, 0.0, NEG).astype(np.float32)
    rbk1 = np.where(m < 32, 0.0, NEG).astype(np.float32)
    kk = np.arange(128)[:, None]
    qq = np.arange(640)[None, :]
    dchunk = qq // 64 - kk // 64
    mask01 = ((dchunk >= 0) & (dchunk <= 8)).astype(np.float32)
    bidx = np.clip(qq - kk, -128, 128) + 128
    return cb, M1T, M2T, rbq1, rbk1, mask01, bidx


def kernel(x, c, norm_g, w_ada, b_ada, w_in, q_norm_g, k_norm_g, rel_bias, lower_bounds, hg_norm_g, w_out):
    x = np.asarray(x, np.float32)
    c = np.asarray(c, np.float32)
    w_ada = np.asarray(w_ada, np.float32)[0]
    b_ada = np.asarray(b_ada, np.float32)[0]
    w_in = np.asarray(w_in, np.float32)[0]
    w_out = np.asarray(w_out, np.float32)[0]
    norm_g = np.asarray(norm_g, np.float32)[0]
    q_norm_g = np.asarray(q_norm_g, np.float32)[0]
    k_norm_g = np.asarray(k_norm_g, np.float32)[0]
    rel_bias = np.asarray(rel_bias, np.float32)[0]
    lower_bounds = np.asarray(lower_bounds, np.float32)
    hg_norm_g = np.asarray(hg_norm_g, np.float32)[0]

    cb, M1T, M2T, rbq1, rbk1, mask01, bidx = _consts()
    if 'nc' not in _NC_CACHE:
        _NC_CACHE['nc'] = build_program()
    nc = _NC_CACHE['nc']

    in_maps = []
    for core in range(8):
        b, hh = core // 2, core % 2
        cfa = np.zeros((128, NF), np.float32)
        cfa[:, CF_M1:CF_M1 + WD] = np.tile(M1T, (1, NH))
        cfa[:, CF_M2:CF_M2 + WD] = np.tile(M2T, (1, NH))
        cfa[:, CF_RB] = rbq1
        cfa[:, CF_RB + 1] = rbk1
        cfa[:, CF_MASK:CF_MASK + 640] = mask01
        cfa[:, CF_OG:CF_OG + 512] = np.tile(hg_norm_g[None, :], (128, 4))
        cfa[:, CF_LB0:CF_LB0 + 512] = np.tile(lower_bounds[0, hh * 512:(hh + 1) * 512][None, :], (128, 1))
        cfa[:, CF_LB1:CF_LB1 + 512] = np.tile(lower_bounds[1, hh * 512:(hh + 1) * 512][None, :], (128, 1))
        cfa[:, CF_NG:CF_NG + 16] = norm_g.reshape(16, 128).T
        cfa[:, CF_GQ] = q_norm_g
        cfa[:, CF_GK] = k_norm_g
        cfa[:, CF_C:CF_C + 16] = c[b].reshape(16, 128).T
        cfa[:, CF_ONE:CF_ONE + 128] = 1.0
        cols = np.concatenate([np.arange(0, 4096), 4096 + hh * 1024 + np.arange(1024)])
        w_ada_c = np.ascontiguousarray(w_ada[:, cols])
        b_ada_c = np.ascontiguousarray(b_ada[cols][None, :])
        w_in_c = np.stack([w_in[:, s * 1024 + hh * 512: s * 1024 + (hh + 1) * 512] for s in range(8)], axis=0)
        rows = []
        for fb in range(8):
            for r_ in range(2):
                if fb < 4:
                    g = 4 * r_ + fb
                    rows.append(np.arange(g * 128, (g + 1) * 128))
                else:
                    g = 4 * r_ + (fb - 4)
                    rows.append(1024 + np.arange(g * 128, (g + 1) * 128))
        rows = np.concatenate(rows)
        w_out_c = np.ascontiguousarray(w_out[rows][:, hh * 1024:(hh + 1) * 1024])
        biasT = np.concatenate([rel_bias[4 * hh + h][bidx] for h in range(4)], axis=1).astype(np.float32)
        in_maps.append({
            "x_b": np.ascontiguousarray(x[b]),
            "x_res": np.ascontiguousarray(x[b][:, hh * 1024:(hh + 1) * 1024]),
            "w_ada_c": w_ada_c,
            "b_ada_c": b_ada_c,
            "w_in_c": np.ascontiguousarray(w_in_c),
            "w_out_c": w_out_c,
            "cbf": cb,
            "cf": cfa,
            "biasT": np.ascontiguousarray(biasT),
        })
    res = run_bass_kernel_spmd(nc, in_maps, core_ids=list(range(8)))
    out = np.zeros((4, T, D), np.float32)
    for core in range(8):
        b, hh = core // 2, core % 2
        out[b][:, hh * 1024:(hh + 1) * 1024] = res.results[core]["out"]
    return out
```

```python
import numpy as np
from contextlib import ExitStack
import concourse.bass as bass
import concourse.mybir as mybir
from concourse.bass_utils import run_bass_kernel_spmd

F32 = mybir.dt.float32
BF16 = mybir.dt.bfloat16
AF = mybir.ActivationFunctionType
ALU = mybir.AluOpType

D = 2048
T = 2048
NT = 16
NCH = 16
NH = 2
WD = NH * 128
NPASS = 2
EPS = 1e-6
NEG = -30000.0

CF_M1 = 0
CF_M2 = CF_M1 + WD
CF_RB = CF_M2 + WD
CF_MASK = CF_RB + 2
CF_OG = CF_MASK + 640
CF_LB0 = CF_OG + 512
CF_LB1 = CF_LB0 + 512
CF_NG = CF_LB1 + 512
CF_GQ = CF_NG + 16
CF_GK = CF_GQ + 1
CF_C = CF_GK + 1
CF_ONE = CF_C + 16
NF = CF_ONE + 128
CB_ID = 0
CB_ONE = 128
CB_L = 256
CB_IND = CB_L + 5 * 128
NB = CB_IND + 2 + 2


class Sched:
    def __init__(self):
        self.ops = []
        self.lastw = {}
        self.readers = {}
        self.gflag = False
        self.off = False
        self.budget = None

    BANK = {'psq': 'P0', 'psf': 'P0', 'psi': 'P1', 'psz': 'P1', 'pe_inter': 'P2', 'pe_kdec': 'P2',
            'pe_q1': 'P3', 'pe_k1': 'P3', 'pe_q2': 'P4', 'psbl': 'P4', 'pt_qi': 'P5', 'pt_q1': 'P5',
            'pt_k1': 'P5', 'pt_q2': 'P6', 'pt_k2': 'P4', 'pgt': 'P6', 'psA1': 'P6', 'pst': 'P6',
            'psA2': 'P7', 'pso': 'P7', 'po0': 'P7', 'po1': 'P7', 'P0v': 'P0', 'P0z': 'P0',
            'P1v': 'P1', 'P1z': 'P1'}

    def add(self, eng, fn, r=(), w=(), kind='c', g=None):
        if self.off:
            return -1
        if self.budget is not None:
            if self.budget <= 0:
                return -1
            self.budget -= 1
        i = len(self.ops)
        r = [self.BANK.get(k, k) for k in r]
        w = [self.BANK.get(k, k) for k in w]
        if (self.gflag if g is None else g):
            r.append('GPH')
        deps = set()
        for k in r:
            lw = self.lastw.get(k)
            if lw is not None:
                deps.add(lw)
        for k in w:
            lw = self.lastw.get(k)
            if lw is not None:
                deps.add(lw)
            rd = self.readers.get(k)
            if rd:
                deps.update(rd[0].values())
                deps.update(rd[1])
        deps.discard(i)
        self.ops.append((eng, fn, deps, kind))
        for k in r:
            rd = self.readers.setdefault(k, ({}, []))
            if kind == 'c':
                rd[0][eng] = i
            else:
                rd[1].append(i)
        for k in w:
            self.lastw[k] = i
            self.readers[k] = ({}, [])
        return i

    def barrier(self):
        pass

    def emit(self, block, sems, dma_sems, cc_sems):
        ops = self.ops
        n = len(ops)
        need = [False] * n
        for i, (eng, fn, deps, kind) in enumerate(ops):
            for d in deps:
                de, _, _, dk = ops[d]
                if de == 'pe' and eng == 'pe' and dk == 'c' and kind == 'c':
                    continue
                need[d] = True
        sig = {}
        pre = {}
        cnt = {}
        rr = 0
        rrp = 0
        dcnt = [0] * len(dma_sems)
        cci = 0
        for i, (eng, fn, deps, kind) in enumerate(ops):
            if kind == 'dma':
                if eng == 'pool':
                    k = 16 + (rrp % 8)
                    rrp += 1
                else:
                    k = rr % 16
                    rr += 1
                if dcnt[k] > 0:
                    pre[i] = (dma_sems[k], 16 * dcnt[k])
                dcnt[k] += 1
                sig[i] = (dma_sems[k], 16 * dcnt[k])
            elif kind == 'cc':
                sig[i] = (cc_sems[cci], 1)
                cci += 1
            elif need[i]:
                cnt[eng] = cnt.get(eng, 0) + 1
                sig[i] = (sems[eng], cnt[eng])
        order = {}
        for i, o in enumerate(ops):
            order.setdefault(o[0], []).append(i)

        def run(engname, e):
            waited = {}
            for i in order.get(engname, []):
                eng, fn, deps, kind = ops[i]
                wl = {}
                for d in deps:
                    if d not in sig:
                        continue
                    de, _, _, dk = ops[d]
                    if de == 'pe' and eng == 'pe' and dk == 'c' and kind == 'c':
                        continue
                    sm, val = sig[d]
                    key = id(sm)
                    if waited.get(key, 0) < val:
                        if key not in wl or wl[key][1] < val:
                            wl[key] = (sm, val)
                if i in pre:
                    sm, val = pre[i]
                    key = id(sm)
                    if waited.get(key, 0) < val:
                        if key not in wl or wl[key][1] < val:
                            wl[key] = (sm, val)
                for key, (sm, val) in wl.items():
                    e.wait_ge(sm, val)
                    waited[key] = val
                ins = fn(e)
                if i in sig:
                    sm, val = sig[i]
                    if kind == 'dma':
                        ins.then_inc(sm, 16)
                    elif kind == 'cc':
                        ins.then_inc(sm)
                    else:
                        ins.then_inc(sm, 1)
            if engname == 'sp':
                for k, sm in enumerate(dma_sems):
                    if dcnt[k] > 0:
                        e.wait_ge(sm, 16 * dcnt[k])
                for k in range(cci):
                    e.wait_ge(cc_sems[k], 1)

        @block.tensor
        def _(e):
            run('pe', e)

        @block.scalar
        def _(e):
            run('act', e)

        @block.vector
        def _(e):
            run('dve', e)

        @block.gpsimd
        def _(e):
            run('pool', e)

        @block.sync
        def _(e):
            run('sp', e)


DEBUG = False
STOP = 99
SUB = 0
CUT = -1


def build_program():
    nc = bass.Bass("TRN2", target_bir_lowering=False)
    S = Sched()
    dbg_h = nc.dram_tensor("dbg_h", [128, NCH, T], BF16) if DEBUG else None

    def din(name, shape):
        return nc.dram_tensor(name, shape, F32, kind="ExternalInput").ap()

    x_b = din("x_b", [T, D])
    x_res = din("x_res", [T, 1024])
    w_ada_c = din("w_ada_c", [D, 5120])
    b_ada_c = din("b_ada_c", [1, 5120])
    w_in_c = din("w_in_c", [8, D, 512])
    w_out_c = din("w_out_c", [D, 1024])
    cbf_d = din("cbf", [128, NB])
    cf_d = din("cf", [128, NF])
    bias_d = din("biasT", [128, 4 * 640])
    out_d = nc.dram_tensor("out", [T, 1024], F32, kind="ExternalOutput").ap()
    ibs = [nc.dram_tensor(f"ib{i}", [128, T], BF16) for i in range(8)]
    obs = [nc.dram_tensor(f"ob{i}", [256, T], BF16) for i in range(8)]

    es = ExitStack()
    with es:
        def sb(name, shape, dt):
            return es.enter_context(nc.sbuf_tensor(name, shape, dt))

        def ps(name):
            return es.enter_context(nc.psum_tensor(name, [128, 512], F32))

        T_h = sb("T_h", [128, NCH, T], BF16)
        TW = [sb(f"TW{i}", [128, NCH, 512], BF16) for i in range(2)]
        T_w = [TW[i // 2][:, :, (i % 2) * 256:(i % 2) * 256 + 256] for i in range(4)]
        cbf = sb("cbf_s", [128, NB], BF16)
        cf = sb("cf_s", [128, NF], F32)
        GSZ = 31 * 1024
        G = sb("G", [128, GSZ], BF16)
        modT = sb("modT", [128, 32], F32)
        sc1 = sb("sc1", [128, 16], F32)
        gate_bc = sb("gate_bc", [128, 1024], F32)
        cact = sb("cact", [128, 16], BF16)
        small = sb("small", [128, 64], F32)
        lb_bc = sb("lb_bc", [128, 512], F32)
        oml_bc = sb("oml_bc", [128, 512], F32)
        rowblk = sb("rowblk", [1, 512], F32)
        bblk = sb("bblk", [1, 512], F32)
        scratch = sb("scratch", [128, 4], F32)
        P = [ps(f"P{i}") for i in range(8)]

        sems = {k: es.enter_context(nc.semaphore("s_" + k)) for k in ('pe', 'act', 'dve', 'pool', 'sp')}
        dma_sems = [es.enter_context(nc.semaphore(f"dq{i}")) for i in range(24)]
        cc_sems = [es.enter_context(nc.semaphore(f"cc{i}")) for i in range(8)]
        block = es.enter_context(nc.Block())

        ident = cbf[:, CB_ID:CB_ID + 128]
        ones_bf = cbf[:, CB_ONE:CB_ONE + 128]

        def Lm(i):
            return cbf[:, CB_L + i * 128: CB_L + (i + 1) * 128]
        ind2 = cbf[:, CB_IND:CB_IND + 2]

        def mm(out, lhsT, rhs, start, stop, r, w):
            S.add('pe', lambda e: e.matmul(out, lhsT, rhs, start=start, stop=stop), r, w)

        def tr(out, in_, r, w):
            S.add('pe', lambda e: e.transpose(out, in_, ident), list(r) + ['cbf'], w)

        def act(out, in_, func, r, w, bias=None, scale=None, accum=None):
            kw = {}
            if bias is not None:
                kw['bias'] = bias
            if scale is not None:
                kw['scale'] = scale
            if accum is not None:
                kw['accum_out'] = accum
            S.add('act', lambda e: e.activation(out, in_, func, **kw), r, w)

        def tt(eng, out, a, b, op, r, w):
            S.add(eng, lambda e: e.tensor_tensor(out, a, b, op), r, w)

        def ts(eng, out, a, s1, s2, op0, op1, r, w):
            if op1 is None:
                S.add(eng, lambda e: e.tensor_scalar(out, a, s1, None, op0=op0), r, w)
            else:
                S.add(eng, lambda e: e.tensor_scalar(out, a, s1, s2, op0=op0, op1=op1), r, w)

        def stt(eng, out, a, sc, b, op0, op1, r, w):
            S.add(eng, lambda e: e.scalar_tensor_tensor(out, a, sc, b, op0=op0, op1=op1), r, w)

        def cp(eng, out, in_, r, w):
            if eng == 'act':
                S.add('act', lambda e: e.copy(out, in_), r, w)
            else:
                S.add(eng, lambda e: e.tensor_copy(out, in_), r, w)

        def rcp(out, in_, r, w):
            S.add('dve', lambda e: e.reciprocal(out, in_), r, w)

        def dma(q, out, in_, r, w, g=None):
            S.add(q, lambda e: e.dma_start(out=out, in_=in_), r, w, kind='dma', g=g)

        def phase_barrier():
            S.add('dve', lambda e: e.memset(scratch[:, 0:1], 0.0), r=(), w=('GPH',), g=False)

        class Carve:
            def __init__(self):
                self.off = 0

            def get(self, free, dt):
                n = 1
                for f in free:
                    n *= f
                nb = n * (2 if dt == BF16 else 4)
                nb = (nb + 3) // 4 * 4
                a = self.off // 2
                assert a + nb // 2 <= GSZ, (a, nb, GSZ)
                v = G[:, a:a + nb // 2]
                self.off += nb
                if dt == F32:
                    v = v.bitcast(F32)
                if len(free) == 2:
                    v = v.rearrange("p (a b) -> p a b", b=free[1])
                elif len(free) == 3:
                    v = v.rearrange("p (a b c) -> p a b c", b=free[1], c=free[2])
                return v

        dma('pool', cbf[:], cbf_d[:, :], [], ['cbf'], g=False)
        dma('sp', cf[:], cf_d[:, :], [], ['cf'], g=False)

        tt('dve', lb_bc[:], cf[:, CF_LB0:CF_LB0 + 512], cf[:, CF_LB1:CF_LB1 + 512], ALU.subtract, ['cf'], ['lb'])
        act(lb_bc[:], lb_bc[:], AF.Exp, ['lb'], ['lb'], scale=-1.0)
        ts('dve', lb_bc[:], lb_bc[:], 1.0, None, ALU.add, None, ['lb'], ['lb'])
        rcp(lb_bc[:], lb_bc[:], ['lb'], ['lb'])
        ts('dve', oml_bc[:], lb_bc[:], -1.0, 1.0, ALU.mult, ALU.add, ['lb'], ['oml'])
        ts('dve', small[:, 0:1], cf[:, CF_GQ:CF_GQ + 1], 128.0 ** -0.5, None, ALU.mult, None, ['cf'], ['gq'])
        cp('dve', small[:, 1:2], cf[:, CF_GK:CF_GK + 1], ['cf'], ['gk'])
        ts('dve', small[:, 2:3], cf[:, CF_GK:CF_GK + 1], 0.0, EPS, ALU.mult, ALU.add, ['cf'], ['epsc'])
        eps_c = small[:, 2:3]

        cT = cf[:, CF_C:CF_C + 16]
        act(small[:, 16:32], cT, AF.Exp, ['cf'], ['ctmp'], scale=-1.0)
        ts('dve', small[:, 16:32], small[:, 16:32], 1.0, None, ALU.add, None, ['ctmp'], ['ctmp'])
        rcp(small[:, 16:32], small[:, 16:32], ['ctmp'], ['ctmp'])
        tt('dve', cact[:], small[:, 16:32], cT, ALU.mult, ['ctmp', 'cf'], ['cact'])
        ones_f = cf[:, CF_ONE:CF_ONE + 128]
        wada_v = w_ada_c.rearrange("(c p) n -> p c n", p=128)
        for blk in range(20):
            wb = T_w[blk % 4]
            wk = f'Tw{blk % 4}'
            for q4 in range(4):
                dma('pool', wb[:, q4 * 4:(q4 + 1) * 4, :], wada_v[:, q4 * 4:(q4 + 1) * 4, blk * 256:(blk + 1) * 256], [], [wk], g=False)
            dma('sp', bblk[0:1, 0:256], b_ada_c[0:1, blk * 256:(blk + 1) * 256], [], ['bblk'], g=False)
            for c in range(NCH):
                mm(P[0][0:1, 0:256], cact[:, c:c + 1], wb[:, c, :], c == 0, c == NCH - 1, [wk, 'cact'], ['P0'])
            tt('dve', rowblk[0:1, 0:256], P[0][0:1, 0:256], bblk[0:1, 0:256], ALU.add, ['P0', 'bblk'], ['rowblk'])
            if blk < 16:
                for s in range(2):
                    col = blk * 2 + s
                    mm(P[1][:, col:col + 1], rowblk[0:1, s * 128:(s + 1) * 128], ones_f[0:1, 0:1], True, True,
                       ['rowblk', 'cf'], ['P1'])
            else:
                gb = blk - 16
                mm(P[2][:, 0:256], ones_f[0:1, 0:128], rowblk[0:1, 0:256], True, True, ['rowblk', 'cf'], ['P2'])
                cp('dve', gate_bc[:, gb * 256:(gb + 1) * 256], P[2][:, 0:256], ['P2'], ['gate'])
            if blk == 15:
                cp('dve', modT[:], P[1][:, 0:32], ['P1'], ['modT'])
        stt('dve', sc1[:], modT[:, 16:32], 1.0, cf[:, CF_NG:CF_NG + 16], ALU.add, ALU.mult, ['modT', 'cf'], ['sc1'])

        if STOP < 1:
            S.off = True
        S.gflag = True
        phase_barrier()
        cv = Carve()
        xs = [cv.get([D], F32) for _ in range(2)]
        xn = cv.get([4, D], BF16)
        junk = cv.get([D], BF16)
        ssum = small[:, 32:48]
        S.add('dve', lambda e: e.memset(ssum, 0.0), [], [f'ss{i}' for i in range(16)])
        rstd = small[:, 48:64]
        for gI in range(4):
            for ti in range(4):
                Tt = gI * 4 + ti
                xb_ = xs[Tt % 2]
                xk = f'xs{Tt % 2}'
                dma('sp', xb_, x_b[Tt * 128:(Tt + 1) * 128, :], [], [xk])
                act(junk, xb_, AF.Square, [xk], ['junk', f'ss{Tt}'], accum=ssum[:, Tt:Tt + 1])
                act(rstd[:, Tt:Tt + 1], ssum[:, Tt:Tt + 1], AF.Ln, [f'ss{Tt}', 'epsc'], [f'rs{Tt}'], bias=eps_c, scale=1.0 / D)
                act(rstd[:, Tt:Tt + 1], rstd[:, Tt:Tt + 1], AF.Exp, [f'rs{Tt}'], [f'rs{Tt}'], scale=-0.5)
                ts('dve', xn[:, ti, :], xb_, rstd[:, Tt:Tt + 1], None, ALU.mult, None, [xk, f'rs{Tt}'], [f'xn{ti}'])
            for c in range(NCH):
                pb = P[c % 4]
                pk = f'P{c % 4}'
                pbv = pb[:, 0:256].bitcast(BF16)
                for ti in range(4):
                    tr(pbv[:, ti * 128:(ti + 1) * 128], xn[:, ti, c * 128:(c + 1) * 128], [f'xn{ti}'], [pk])
                dst = T_h[:, c, gI * 512:(gI + 1) * 512]
                if c % 2 == 0:
                    act(dst, pbv, AF.Identity, [pk, 'sc1', 'modT'], [f'h{c}'], bias=modT[:, c:c + 1], scale=sc1[:, c:c + 1])
                else:
                    ts('dve', dst, pbv, sc1[:, c:c + 1], modT[:, c:c + 1], ALU.mult, ALU.add, [pk, 'sc1', 'modT'], [f'h{c}'])
        hkeys = [f'h{c}' for c in range(NCH)]
        if DEBUG:
            dma('sp', dbg_h[:, :, :], T_h[:, :, :], hkeys, ['dbgh'])

        def load_slab(slot, s_idx, p):
            src = w_in_c[s_idx].rearrange("(c p) n -> p c n", p=128)
            for q4 in range(4):
                dma('pool', T_w[slot][:, q4 * 4:(q4 + 1) * 4, :], src[:, q4 * 4:(q4 + 1) * 4, p * WD:(p + 1) * WD], [], [f'Tw{slot}'], g=False)

        if STOP < 2:
            S.off = True
        if CUT >= 0:
            S.budget = CUT
        phase_barrier()
        cv = Carve()
        f32t = {k: cv.get([WD], F32) for k in ('tq', 'qf', 'tf', 'ff', 'logf', 'kk', 'tz', 'gz2', 'E', 'A1', 'A2')}
        bft = {k: cv.get([WD], BF16) for k in ('hi', 'lo', 'vb', 'qi', 'kd', 'q1', 'k1', 'q2', 'k2', 'ghg')}
        ATb = cv.get([NH, 128], BF16)
        trT = {k: cv.get([NH, 128], BF16) for k in ('qiT0', 'qiT1', 'q1T', 'k1T', 'q2T', 'k2T', 'ghgT')}
        Sf = cv.get([NH, 128], F32)
        Sb = cv.get([NH, 128], BF16)
        Dd = cv.get([2 * NH], F32)
        ssq = cv.get([NH], F32)
        rso = cv.get([NH], F32)
        junk2 = cv.get([128], BF16)
        rbq1 = cf[:, CF_RB:CF_RB + 1]
        rbk1 = cf[:, CF_RB + 1:CF_RB + 2]
        M1 = cf[:, CF_M1:CF_M1 + WD]
        M2 = cf[:, CF_M2:CF_M2 + WD]

        def sigm(dst, src_ps, tmpk, tmp, pk, outk):
            act(tmp, src_ps, AF.Exp, [pk], [tmpk], scale=-1.0)
            ts('dve', tmp, tmp, 1.0, None, ALU.add, None, [tmpk], [tmpk])
            rcp(dst, tmp, [tmpk], [outk])

        for p in range(NPASS):
            for s in range(4):
                load_slab(s, 4 + s, p)
            S.add('dve', lambda e: e.memset(trT['qiT0'][:, :, :], 0.0), [], ['qiT0'])
            S.add('dve', lambda e: e.memset(trT['qiT1'][:, :, :], 0.0), [], ['qiT1'])
            S.add('dve', lambda e: e.memset(Sf[:, :, :], 0.0), [], ['Sf'])
            S.add('dve', lambda e: e.memset(Sb[:, :, :], 0.0), [], ['Sb'])
            lbp = lb_bc[:, p * WD:(p + 1) * WD]
            omlp = oml_bc[:, p * WD:(p + 1) * WD]
            ogp = cf[:, CF_OG + p * WD: CF_OG + (p + 1) * WD]
            NTL = NT if SUB < 3 else SUB - 2
            psq, psf = P[0][:, 0:WD], P[0][:, 256:256 + WD]
            psi, psz = P[1][:, 0:WD], P[1][:, 256:256 + WD]

            def hproj(Tn):
                tk = slice(Tn * 128, (Tn + 1) * 128)
                for (dst, dk, bi) in ((P[0][:, :], 'psq', 0), (P[1][:, :], 'psi', 1)):
                    for c in range(NCH):
                        mm(dst, T_h[:, c, tk], TW[bi][:, c, :], c == 0, c == NCH - 1,
                           [f'h{c}', f'Tw{2 * bi}', f'Tw{2 * bi + 1}'], [dk])
            hproj(0)
            for Tt in range(NTL):
                tok = slice(Tt * 128, (Tt + 1) * 128)
                sigm(f32t['tq'], psq, 'tq', f32t['tq'], 'psq', 'tq')
                tt('dve', f32t['qf'], psq, f32t['tq'], ALU.mult, ['psq', 'tq'], ['qf'])
                sigm(f32t['tf'], psf, 'tf', f32t['tf'], 'psf', 'tf')
                tt('dve', f32t['ff'], f32t['tf'], omlp, ALU.mult, ['tf', 'oml'], ['ff'])
                tt('dve', f32t['ff'], f32t['ff'], lbp, ALU.add, ['ff', 'lb'], ['ff'])
                act(f32t['logf'], f32t['ff'], AF.Ln, ['ff'], ['logf'])
                ts('dve', f32t['kk'], f32t['ff'], -1.0, 1.0, ALU.mult, ALU.add, ['ff'], ['kk'])
                cp('dve', bft['hi'], f32t['logf'], ['logf'], ['hi'])
                tt('dve', bft['lo'], f32t['logf'], bft['hi'], ALU.subtract, ['logf', 'hi'], ['lo'])
                cp('act', bft['vb'], psi, ['psi'], ['vb'])
                sigm(f32t['tz'], psz, 'tz', f32t['tz'], 'psz', 'tz')
                tt('dve', f32t['gz2'], psz, f32t['tz'], ALU.mult, ['psz', 'tz'], ['gz2'])
                tt('dve', f32t['gz2'], f32t['gz2'], ogp, ALU.mult, ['gz2', 'cf'], ['gz2'])
                if Tt + 1 < NTL:
                    hproj(Tt + 1)
                pes = {'inter': (P[2][:, 0:WD], 0), 'kdec': (P[2][:, 256:256 + WD], 1), 'q1': (P[3][:, 0:WD], 2),
                       'k1': (P[3][:, 256:256 + WD], 3), 'q2': (P[4][:, 0:WD], 4)}
                for nm, (pt, li) in pes.items():
                    mm(pt, Lm(li), bft['hi'], True, False, ['hi', 'cbf'], ['pe_' + nm])
                    mm(pt, Lm(li), bft['lo'], False, True, ['lo', 'cbf'], ['pe_' + nm])
                psbl = P[4][:, 256:256 + 2 * NH]
                for h in range(NH):
                    mm(psbl[:, 2 * h:2 * h + 2], bft['hi'][:, h * 128:(h + 1) * 128], ind2, True, False, ['hi', 'cbf'], ['psbl'])
                    mm(psbl[:, 2 * h:2 * h + 2], bft['lo'][:, h * 128:(h + 1) * 128], ind2, False, True, ['lo', 'cbf'], ['psbl'])
                act(Dd, psbl, AF.Exp, ['psbl'], ['Dd'])

                def expmul(pe_nm, src_k, src, dst_k, bias=None, scale=None):
                    act(f32t['E'], pes[pe_nm][0], AF.Exp, ['pe_' + pe_nm, 'cf'], ['E'], bias=bias, scale=scale)
                    tt('dve', bft[dst_k], src, f32t['E'], ALU.mult, [src_k, 'E'], [dst_k])
                expmul('inter', 'qf', f32t['qf'], 'qi')
                expmul('kdec', 'kk', f32t['kk'], 'kd')
                expmul('q1', 'qf', f32t['qf'], 'q1', bias=rbq1)
                expmul('k1', 'kk', f32t['kk'], 'k1', bias=rbk1)
                expmul('q2', 'qf', f32t['qf'], 'q2')
                expmul('q2', 'kk', f32t['kk'], 'k2', scale=-1.0)
                P5b = P[5][:, :].bitcast(BF16)
                P6b = P[6][:, 0:128].bitcast(BF16)
                P4b = P[4][:, 384:512].bitcast(BF16)
                tslots = {'qi': P5b[:, 0:WD], 'q1': P5b[:, 256:256 + WD], 'k1': P5b[:, 512:512 + WD],
                          'q2': P6b[:, 0:WD], 'k2': P4b[:, 0:WD]}
                for nm, pt in tslots.items():
                    for h in range(NH):
                        tr(pt[:, h * 128:(h + 1) * 128], bft[nm][:, h * 128:(h + 1) * 128], [nm], ['pt_' + nm])
                ptq = tslots['qi'].rearrange("p (a b) -> p a b", b=128)
                cp('dve', trT['qiT0'][:, :, 0:64], ptq[:, :, 0:64], ['pt_qi'], ['qiT0'])
                cp('dve', trT['qiT1'][:, :, 64:128], ptq[:, :, 64:128], ['pt_qi'], ['qiT1'])
                for nm in ('q1', 'k1', 'q2', 'k2'):
                    eng = 'act' if nm in ('q1', 'q2') else 'dve'
                    cp(eng, trT[nm + 'T'][:, :, :], tslots[nm].rearrange("p (a b) -> p a b", b=128), ['pt_' + nm], [nm + 'T'])
                psA1 = P[6][:, 256:256 + WD]
                psA2 = P[7][:, 0:WD]
                pso = P[7][:, 256:256 + WD]
                for h in range(NH):
                    hs = slice(h * 128, (h + 1) * 128)
                    mm(psA1[:, hs], trT['k1T'][:, h, :], trT['q1T'][:, h, :], True, True, ['k1T', 'q1T'], ['psA1'])
                    mm(psA2[:, hs], trT['k2T'][:, h, :], trT['q2T'][:, h, :], True, True, ['k2T', 'q2T'], ['psA2'])
                tt('dve', f32t['A1'], psA1, M1, ALU.mult, ['psA1', 'cf'], ['A1'])
                tt('dve', f32t['A2'], psA2, M2, ALU.mult, ['psA2', 'cf'], ['A2'])
                tt('dve', ATb[:, :, :], f32t['A1'].rearrange("p (a b) -> p a b", b=128),
                   f32t['A2'].rearrange("p (a b) -> p a b", b=128), ALU.add, ['A1', 'A2'], ['ATb'])
                psU = [P[2][:, 0:WD], P[2][:, 256:256 + WD]]
                for h in range(NH):
                    hs = slice(h * 128, (h + 1) * 128)
                    mm(pso[:, hs], ATb[:, h, :], bft['vb'][:, hs], True, False, ['ATb', 'vb'], ['pso'])
                    mm(pso[:, hs], trT['qiT0'][:, h, :], Sb[:, h, :], False, False, ['qiT0', f'Sb{h}'], ['pso'])
                    mm(psU[0][:, hs], bft['kd'][0:64, hs], bft['vb'][0:64, hs], True, True, ['kd', 'vb'], ['pe_inter'])
                    stt('dve', Sf[:, h, :], Sf[:, h, :], Dd[:, 2 * h:2 * h + 1], psU[0][:, hs], ALU.mult, ALU.add,
                        ['pe_inter', 'Dd', f'Sf{h}'], [f'Sf{h}'])
                    cp('act', Sb[:, h, :], Sf[:, h, :], [f'Sf{h}'], [f'Sb{h}'])
                    mm(pso[:, hs], trT['qiT1'][:, h, :], Sb[:, h, :], False, True, ['qiT1', f'Sb{h}'], ['pso'])
                    mm(psU[1][:, hs], bft['kd'][64:128, hs], bft['vb'][64:128, hs], True, True, ['kd', 'vb'], ['pe_kdec'])
                    stt('dve', Sf[:, h, :], Sf[:, h, :], Dd[:, 2 * h + 1:2 * h + 2], psU[1][:, hs], ALU.mult, ALU.add,
                        ['pe_kdec', 'Dd', f'Sf{h}'], [f'Sf{h}'])
                    cp('act', Sb[:, h, :], Sf[:, h, :], [f'Sf{h}'], [f'Sb{h}'])
                S.add('dve', lambda e: e.memset(ssq, 0.0), [], ['ssq'])
                for h in range(NH):
                    hs = slice(h * 128, (h + 1) * 128)
                    act(junk2, pso[:, hs], AF.Square, ['pso'], ['junk2', 'ssq'], accum=ssq[:, h:h + 1])
                act(rso, ssq, AF.Ln, ['ssq', 'epsc'], ['rso'], bias=eps_c, scale=1.0 / 128)
                act(rso, rso, AF.Exp, ['rso'], ['rso'], scale=-0.5)
                for h in range(NH):
                    hs = slice(h * 128, (h + 1) * 128)
                    stt('dve', bft['ghg'][:, hs], pso[:, hs], rso[:, h:h + 1], f32t['gz2'][:, hs], ALU.mult, ALU.mult,
                        ['pso', 'rso', 'gz2'], ['ghg'])
                pgt = P[6][:, 128:256].bitcast(BF16)
                for h in range(NH):
                    tr(pgt[:, h * 128:(h + 1) * 128], bft['ghg'][:, h * 128:(h + 1) * 128], ['ghg'], ['pgt'])
                cp('act', trT['ghgT'][:, :, :], pgt.rearrange("p (a b) -> p a b", b=128), ['pgt'], ['ghgT'])
                for h in range(NH):
                    fb = 4 + p * NH + h
                    if SUB < 2:
                        dma('sp', ibs[fb][:, Tt * 128:(Tt + 1) * 128], trT['ghgT'][:, h, :], ['ghgT'], [f'ib{fb}'])
            for h in range(NH):
                fb = 4 + p * NH + h
                if SUB >= 1:
                    continue
                S.add('pool', lambda e, fb=fb: e.collective_compute(
                    "AllGather", ALU.bypass, replica_groups=[[0, 1], [2, 3], [4, 5], [6, 7]],
                    ins=[ibs[fb].ap().opt()], outs=[obs[fb].ap().opt()]), [f'ib{fb}'], [f'ob{fb}'], kind='cc', g=False)

        if STOP < 3:
            S.off = True
        phase_barrier()
        cv = Carve()
        qnT = cv.get([NH, T], BF16)
        knT = cv.get([NH, T], BF16)
        Vaug = cv.get([NT, NH, 130], BF16)
        Zs = cv.get([NT, WD], BF16)
        ring = [cv.get([640], BF16) for _ in range(6)]
        EB = cv.get([NH, 640], BF16)
        bstage = cv.get([640], F32)
        qraw = cv.get([512], F32)
        sqb = cv.get([512], BF16)
        rq = cv.get([512], F32)
        tzz = cv.get([WD], F32)
        stageA = cv.get([128], BF16)
        GTh = cv.get([T], BF16)
        rc = cv.get([2], F32)
        maskf = cf[:, CF_MASK:CF_MASK + 640]
        for p in range(NPASS):
            for s in range(4):
                load_slab(s, s, p)
            S.add('dve', lambda e: e.memset(Vaug[:, :, :, 128:130], 1.0), [], ['Vaug'])
            for h in range(NH):
                hg = p * NH + h
                dma('sp', bstage, bias_d[:, hg * 640:(hg + 1) * 640], [], ['bstage'])
                act(bstage, bstage, AF.Exp, ['bstage'], ['bstage'])
                tt('dve', EB[:, h, :], bstage, maskf, ALU.mult, ['bstage', 'cf'], ['EB'])
            for Tt in range(NT):
                tok = slice(Tt * 128, (Tt + 1) * 128)
                pb = P[Tt % 2]
                pk = f'P{Tt % 2}'
                for c in range(NCH):
                    mm(pb[:, :], T_h[:, c, tok], TW[1][:, c, :], c == 0, c == NCH - 1, [f'h{c}', 'Tw2', 'Tw3'], [pk + 'v'])
                cp('act', Vaug[:, Tt, :, 0:128], pb[:, 0:WD].rearrange("p (a b) -> p a b", b=128), [pk + 'v'], ['Vaug'])
                sigm(tzz, pb[:, 256:256 + WD], 'tzz', tzz, pk + 'z', 'tzz')
                tt('dve', Zs[:, Tt, :], pb[:, 256:256 + WD], tzz, ALU.mult, [pk + 'z', 'tzz'], ['Zs'])
            for h in range(NH):
                for (slot, dstT, gcol, nmk) in ((0, qnT, small[:, 0:1], 'qnT'), (1, knT, small[:, 1:2], 'knT')):
                    for nb in range(4):
                        tb = slice(nb * 512, (nb + 1) * 512)
                        pq = P[2 + (nb % 2)]
                        pqk = f'P{2 + (nb % 2)}'
                        for c in range(NCH):
                            mm(pq[:, :], T_w[slot][:, c, h * 128:(h + 1) * 128], T_h[:, c, tb], c == 0, c == NCH - 1,
                               [f'h{c}', f'Tw{slot}'], [pqk])
                        cp('act', qraw, pq[:, :], [pqk], ['qraw'])
                        act(sqb, pq[:, :], AF.Square, [pqk], ['sqb'])
                        mm(P[4][:, :], ones_bf, sqb, True, True, ['sqb', 'cbf'], ['P4'])
                        act(rq, P[4][:, :], AF.Ln, ['P4', 'epsc'], ['rq'], bias=eps_c, scale=1.0 / 128)
                        act(rq, rq, AF.Exp, ['rq'], ['rq'], scale=-0.5)
                        stt('dve', dstT[:, h, tb], qraw, gcol, rq, ALU.mult, ALU.mult, ['qraw', 'rq', 'gq', 'gk'], [nmk + str(h)])
            for h in range(NH):
                fb = p * NH + h
                def smm(jn):
                    Wn = min(640, T - 128 * jn)
                    mm(P[5][:, 0:min(Wn, 512)], knT[:, h, jn * 128:(jn + 1) * 128], qnT[:, h, jn * 128:jn * 128 + min(Wn, 512)],
                       True, True, [f'knT{h}', f'qnT{h}'], ['P5'])
                    if Wn > 512:
                        mm(P[6][:, 0:128], knT[:, h, jn * 128:(jn + 1) * 128], qnT[:, h, jn * 128 + 512:jn * 128 + 640],
                           True, True, [f'knT{h}', f'qnT{h}'], ['P6'])
                smm(0)
                for j in range(NT):
                    W = min(640, T - 128 * j)
                    W0 = min(W, 512)
                    rj = ring[j % 6]
                    rk = f'ring{j % 6}'
                    act(rj[:, 0:W0], P[5][:, 0:W0], AF.Exp, ['P5'], [rk])
                    if W > 512:
                        act(rj[:, 512:640], P[6][:, 0:128], AF.Exp, ['P6'], [rk])
                    if j + 1 < NT:
                        smm(j + 1)
                    tt('dve', rj[:, 0:W], rj[:, 0:W], EB[:, h, 0:W], ALU.mult, [rk, 'EB'], [rk])
                    po = P[7][:, 0:129] if j % 2 == 0 else P[7][:, 256:385]
                    pok = f'po{j % 2}'
                    t0 = max(0, j - 4)
                    for t in range(t0, j + 1):
                        mm(po, ring[t % 6][:, (j - t) * 128:(j - t + 1) * 128], Vaug[:, t, h, 0:129], t == t0, t == j,
                           [f'ring{t % 6}', 'Vaug'], [pok])
                    rcp(rc[:, 0:1], po[:, 128:129], [pok], ['rc'])
                    stt('dve', stageA, po[:, 0:128], rc[:, 0:1], Zs[:, j, h * 128:(h + 1) * 128], ALU.mult, ALU.mult,
                        [pok, 'rc', 'Zs'], ['stageA'])
                    pst = P[6][:, 256:320].bitcast(BF16)
                    tr(pst, stageA, ['stageA'], ['pst'])
                    cp('act', GTh[:, j * 128:(j + 1) * 128], pst, ['pst'], ['GTh'])
                dma('sp', ibs[fb][:, :], GTh, ['GTh'], [f'ib{fb}'])
                S.add('pool', lambda e, fb=fb: e.collective_compute(
                    "AllGather", ALU.bypass, replica_groups=[[0, 1], [2, 3], [4, 5], [6, 7]],
                    ins=[ibs[fb].ap().opt()], outs=[obs[fb].ap().opt()]), [f'ib{fb}'], [f'ob{fb}'], kind='cc', g=False)

        if STOP < 4:
            S.off = True
        phase_barrier()
        cv = Carve()
        xr = [cv.get([1024], F32) for _ in range(2)]
        ot = [cv.get([512], F32) for _ in range(2)]
        for fb in range(8):
            for r_ in range(2):
                ch = fb * 2 + r_
                dma('sp', T_h[:, ch, :], obs[fb][r_ * 128:(r_ + 1) * 128, :], [f'ob{fb}'], [f'h{ch}'])
        wo_v = w_out_c.rearrange("(c p) n -> p c n", p=128)
        for q_ in range(4):
            for q4 in range(4):
                dma('pool', T_w[q_][:, q4 * 4:(q4 + 1) * 4, :], wo_v[:, q4 * 4:(q4 + 1) * 4, q_ * 256:(q_ + 1) * 256], [], [f'Tw{q_}'], g=False)
        k_ = 0
        for Tt in range(NT):
            tok = slice(Tt * 128, (Tt + 1) * 128)
            xrb = xr[Tt % 2]
            xrk = f'xr{Tt % 2}'
            dma('sp', xrb, x_res[tok, :], [], [xrk])
            for cb in range(2):
                pb = P[k_ % 4]
                pk = f'P{k_ % 4}'
                ob_ = ot[k_ % 2]
                ok_ = f'ot{k_ % 2}'
                k_ += 1
                for ch in range(NCH):
                    mm(pb[:, :], T_h[:, ch, tok], TW[cb][:, ch, :], ch == 0, ch == NCH - 1,
                       [f'h{ch}', f'Tw{2 * cb}', f'Tw{2 * cb + 1}'], [pk])
                tt('dve', ob_, pb[:, :], gate_bc[:, cb * 512:(cb + 1) * 512], ALU.mult, [pk, 'gate'], [ok_])
                tt('dve', ob_, ob_, xrb[:, cb * 512:(cb + 1) * 512], ALU.add, [ok_, xrk], [ok_])
                dma('sp', out_d[tok, cb * 512:(cb + 1) * 512], ob_, [ok_], ['outd'])

        S.emit(block, sems, dma_sems, cc_sems)
    return nc


_NC_CACHE = {}


def _consts():
    idx = np.arange(128)
    ch = idx // 64
    m = idx % 64
    s_ = idx[:, None]
    t_ = idx[None, :]
    same = (ch[:, None] == ch[None, :])
    ms = m[:, None]
    mt = m[None, :]
    L_inter = same & (s_ <= t_)
    L_kdec = same & (s_ > t_)
    L_q1 = same & (mt >= 32) & (ms >= 32) & (ms <= mt)
    L_k1 = same & (mt < 32) & (ms > mt) & (ms <= 31)
    blk = idx // 32
    sameb = blk[:, None] == blk[None, :]
    L_q2 = sameb & (s_ <= t_)
    cb = np.zeros((128, NB), np.float32)
    cb[:, CB_ID:CB_ID + 128] = np.eye(128)
    cb[:, CB_ONE:CB_ONE + 128] = 1.0
    for i, L in enumerate((L_inter, L_kdec, L_q1, L_k1, L_q2)):
        cb[:, CB_L + i * 128:CB_L + (i + 1) * 128] = L.astype(np.float32)
    cb[:, CB_IND] = (ch == 0)
    cb[:, CB_IND + 1] = (ch == 1)
    M1T = (same & (ms < 32) & (mt >= 32)).astype(np.float32)
    M2T = (sameb & (s_ <= t_)).astype(np.float32)
    rbq1 = np.where(m >= 32, 0.0, NEG).astype(np.float32)
    rbk1 = np.where(m < 32, 0.0, NEG).astype(np.float32)
    kk = np.arange(128)[:, None]
    qq = np.arange(640)[None, :]
    dchunk = qq // 64 - kk // 64
    mask01 = ((dchunk >= 0) & (dchunk <= 8)).astype(np.float32)
    bidx = np.clip(qq - kk, -128, 128) + 128
    return cb, M1T, M2T, rbq1, rbk1, mask01, bidx


def kernel(x, c, norm_g, w_ada, b_ada, w_in, q_norm_g, k_norm_g, rel_bias, lower_bounds, hg_norm_g, w_out):
    x = np.asarray(x, np.float32)
    c = np.asarray(c, np.float32)
    w_ada = np.asarray(w_ada, np.float32)[0]
    b_ada = np.asarray(b_ada, np.float32)[0]
    w_in = np.asarray(w_in, np.float32)[0]
    w_out = np.asarray(w_out, np.float32)[0]
    norm_g = np.asarray(norm_g, np.float32)[0]
    q_norm_g = np.asarray(q_norm_g, np.float32)[0]
    k_norm_g = np.asarray(k_norm_g, np.float32)[0]
    rel_bias = np.asarray(rel_bias, np.float32)[0]
    lower_bounds = np.asarray(lower_bounds, np.float32)
    hg_norm_g = np.asarray(hg_norm_g, np.float32)[0]

    cb, M1T, M2T, rbq1, rbk1, mask01, bidx = _consts()
    if 'nc' not in _NC_CACHE:
        _NC_CACHE['nc'] = build_program()
    nc = _NC_CACHE['nc']

    in_maps = []
    for core in range(8):
        b, hh = core // 2, core % 2
        cfa = np.zeros((128, NF), np.float32)
        cfa[:, CF_M1:CF_M1 + WD] = np.tile(M1T, (1, NH))
        cfa[:, CF_M2:CF_M2 + WD] = np.tile(M2T, (1, NH))
        cfa[:, CF_RB] = rbq1
        cfa[:, CF_RB + 1] = rbk1
        cfa[:, CF_MASK:CF_MASK + 640] = mask01
        cfa[:, CF_OG:CF_OG + 512] = np.tile(hg_norm_g[None, :], (128, 4))
        cfa[:, CF_LB0:CF_LB0 + 512] = np.tile(lower_bounds[0, hh * 512:(hh + 1) * 512][None, :], (128, 1))
        cfa[:, CF_LB1:CF_LB1 + 512] = np.tile(lower_bounds[1, hh * 512:(hh + 1) * 512][None, :], (128, 1))
        cfa[:, CF_NG:CF_NG + 16] = norm_g.reshape(16, 128).T
        cfa[:, CF_GQ] = q_norm_g
        cfa[:, CF_GK] = k_norm_g
        cfa[:, CF_C:CF_C + 16] = c[b].reshape(16, 128).T
        cfa[:, CF_ONE:CF_ONE + 128] = 1.0
        cols = np.concatenate([np.arange(0, 4096), 4096 + hh * 1024 + np.arange(1024)])
        w_ada_c = np.ascontiguousarray(w_ada[:, cols])
        b_ada_c = np.ascontiguousarray(b_ada[cols][None, :])
        w_in_c = np.stack([w_in[:, s * 1024 + hh * 512: s * 1024 + (hh + 1) * 512] for s in range(8)], axis=0)
        rows = []
        for fb in range(8):
            for r_ in range(2):
                if fb < 4:
                    g = 4 * r_ + fb
                    rows.append(np.arange(g * 128, (g + 1) * 128))
                else:
                    g = 4 * r_ + (fb - 4)
                    rows.append(1024 + np.arange(g * 128, (g + 1) * 128))
        rows = np.concatenate(rows)
        w_out_c = np.ascontiguousarray(w_out[rows][:, hh * 1024:(hh + 1) * 1024])
        biasT = np.concatenate([rel_bias[4 * hh + h][bidx] for h in range(4)], axis=1).astype(np.float32)
        in_maps.append({
            "x_b": np.ascontiguousarray(x[b]),
            "x_res": np.ascontiguousarray(x[b][:, hh * 1024:(hh + 1) * 1024]),
            "w_ada_c": w_ada_c,
            "b_ada_c": b_ada_c,
            "w_in_c": np.ascontiguousarray(w_in_c),
            "w_out_c": w_out_c,
            "cbf": cb,
            "cf": cfa,
            "biasT": np.ascontiguousarray(biasT),
        })
    res = run_bass_kernel_spmd(nc, in_maps, core_ids=list(range(8)))
    out = np.zeros((4, T, D), np.float32)
    for core in range(8):
        b, hh = core // 2, core % 2
        out[b][:, hh * 1024:(hh + 1) * 1024] = res.results[core]["out"]
    return out
```
